# Optimizing a Trainium2 kernel written in Bass

```python
import jax, jax.numpy as jnp
from jax import lax
import numpy as np

D_MODEL = 1024
BATCH = 2
SEQ = 8192
DEPTH = 1

PLE_DIM = 256
CONV_HEADS = 8
CONV_HEAD_DIM = 64
CONV_WIDTH = CONV_HEADS * CONV_HEAD_DIM
CONV_K = 3
MLA_HEADS = 8
Q_LORA = 256
KV_LORA = 128
QK_NOPE = 64
QK_ROPE = 32
V_HEAD = 64
MLA_WIDTH = MLA_HEADS * V_HEAD
D_MIX = CONV_WIDTH + MLA_WIDTH
D_IN = 3 * CONV_WIDTH + Q_LORA + KV_LORA + QK_ROPE
D_FF = 2816
FFN_CONV_K = 3
Q_BLOCK = 128
ROPE_THETA = 10000.0
EPS = 1e-6

kernel_name = "hybrid_shortconv_mla_convffn_ple"


def rmsnorm(x, g):
    xf = x.astype(jnp.float32)
    y = xf * lax.rsqrt(jnp.mean(xf * xf, axis=-1, keepdims=True) + EPS)
    return (y * g.astype(jnp.float32)).astype(x.dtype)


def dwconv_centred(x, w, b=None):
    s = x.shape[1]
    xp = jnp.pad(x, ((0, 0), (1, 1), (0, 0)))
    y = xp[:, 0:s] * w[0] + xp[:, 1:s + 1] * w[1] + xp[:, 2:s + 2] * w[2]
    if b is not None:
        y = y + b
    return y


def rope_tables(s, dim, dtype):
    pos = jnp.arange(s, dtype=jnp.float32)
    inv_freq = ROPE_THETA ** (-jnp.arange(0, dim, 2, dtype=jnp.float32) / dim)
    ang = pos[:, None] * inv_freq[None, :]
    ang = jnp.concatenate([ang, ang], axis=-1)
    return jnp.cos(ang).astype(dtype), jnp.sin(ang).astype(dtype)


def rotate_half(x):
    x1, x2 = jnp.split(x, 2, axis=-1)
    return jnp.concatenate([-x2, x1], axis=-1)


def mla_attention(q_nope, q_rope, k_nope, k_rope, v):
    b, s, h, _ = q_nope.shape
    nb = s // Q_BLOCK
    scale = (QK_NOPE + QK_ROPE) ** -0.5
    qn = q_nope.reshape(b, nb, Q_BLOCK, h, QK_NOPE).transpose(1, 0, 2, 3, 4)
    qr = q_rope.reshape(b, nb, Q_BLOCK, h, QK_ROPE).transpose(1, 0, 2, 3, 4)

    def block(args):
        qn_b, qr_b = args
        sc = (jnp.einsum('bqhd,bkhd->bhqk', qn_b, k_nope)
              + jnp.einsum('bqhr,bkr->bhqk', qr_b, k_rope))
        probs = jax.nn.softmax(sc.astype(jnp.float32) * scale, axis=-1).astype(v.dtype)
        return jnp.einsum('bhqk,bkhd->bqhd', probs, v)

    out = lax.map(block, (qn, qr))
    return out.transpose(1, 0, 2, 3, 4).reshape(b, s, h * V_HEAD)


def setup_inputs(seed: int = 0) -> dict:
    key = jax.random.key(seed)
    ks = jax.random.split(key, 20)
    f32 = jnp.float32

    def nrm(k, shape, fan_in):
        return jax.random.normal(k, shape, f32) * (fan_in ** -0.5)

    def gain(k, dim):
        return 1.0 + 0.01 * jax.random.normal(k, (DEPTH, dim), f32)

    return {
        "x": jax.random.normal(ks[0], (BATCH, SEQ, D_MODEL), f32),
        "p": jax.random.normal(ks[1], (DEPTH, BATCH, SEQ, PLE_DIM), f32),
        "norm_mix_g": gain(ks[2], D_MODEL),
        "w_in": nrm(ks[3], (DEPTH, D_MODEL, D_IN), D_MODEL),
        "conv_w": nrm(ks[4], (DEPTH, CONV_K, CONV_WIDTH), CONV_K),
        "q_norm_g": gain(ks[5], Q_LORA),
        "w_uq": nrm(ks[6], (DEPTH, Q_LORA, MLA_HEADS * (QK_NOPE + QK_ROPE)), Q_LORA),
        "kv_norm_g": gain(ks[7], KV_LORA),
        "w_ukv": nrm(ks[8], (DEPTH, KV_LORA, MLA_HEADS * (QK_NOPE + V_HEAD)), KV_LORA),
        "w_o": nrm(ks[9], (DEPTH, D_MIX, D_MODEL), D_MIX),
        "norm_ffn_g": gain(ks[10], D_MODEL),
        "w_up": nrm(ks[11], (DEPTH, D_MODEL, 2 * D_FF), D_MODEL),
        "ffn_conv_w": nrm(ks[12], (DEPTH, FFN_CONV_K, 2 * D_FF), FFN_CONV_K),
        "ffn_conv_b": 0.01 * jax.random.normal(ks[13], (DEPTH, 2 * D_FF), f32),
        "w_down": nrm(ks[14], (DEPTH, D_FF, D_MODEL), D_FF),
        "ple_norm_g": gain(ks[15], D_MODEL),
        "w_ple_gate": nrm(ks[16], (DEPTH, D_MODEL, D_MODEL), D_MODEL),
        "w_ple_proj": nrm(ks[17], (DEPTH, PLE_DIM, D_MODEL), PLE_DIM),
        "final_norm_g": 1.0 + 0.01 * jax.random.normal(ks[18], (D_MODEL,), f32),
    }


def reference(x, p, norm_mix_g, w_in, conv_w, q_norm_g, w_uq, kv_norm_g, w_ukv, w_o,
              norm_ffn_g, w_up, ffn_conv_w, ffn_conv_b, w_down, ple_norm_g,
              w_ple_gate, w_ple_proj, final_norm_g):
    b, s, _ = x.shape
    cos, sin = rope_tables(s, QK_ROPE, x.dtype)
    split_pts = np.cumsum([CONV_WIDTH, CONV_WIDTH, CONV_WIDTH, Q_LORA, KV_LORA])

    for i in range(DEPTH):
        h = rmsnorm(x, norm_mix_g[i])
        z = h @ w_in[i]
        xc, bg, cg, q_lat, kv_lat, k_r = jnp.split(z, split_pts, axis=-1)

        y_conv = bg * dwconv_centred(cg * xc, conv_w[i])

        q = (rmsnorm(q_lat, q_norm_g[i]) @ w_uq[i]).reshape(b, s, MLA_HEADS, QK_NOPE + QK_ROPE)
        q_nope, q_rope = q[..., :QK_NOPE], q[..., QK_NOPE:]
        q_rope = q_rope * cos[None, :, None, :] + rotate_half(q_rope) * sin[None, :, None, :]
        kv = (rmsnorm(kv_lat, kv_norm_g[i]) @ w_ukv[i]).reshape(b, s, MLA_HEADS, QK_NOPE + V_HEAD)
        k_nope, v = kv[..., :QK_NOPE], kv[..., QK_NOPE:]
        k_rope = k_r * cos[None] + rotate_half(k_r) * sin[None]
        y_mla = mla_attention(q_nope, q_rope, k_nope, k_rope, v)

        x = x + jnp.concatenate([y_conv, y_mla], axis=-1) @ w_o[i]

        hf = rmsnorm(x, norm_ffn_g[i])
        a = dwconv_centred(hf @ w_up[i], ffn_conv_w[i], ffn_conv_b[i])
        g, u = jnp.split(a, 2, axis=-1)
        x = x + (jax.nn.silu(g) * u) @ w_down[i]

        gate = jax.nn.sigmoid(rmsnorm(x, ple_norm_g[i]) @ w_ple_gate[i])
        x = x + gate * (p[i] @ w_ple_proj[i])

    return rmsnorm(x, final_norm_g)
```

```python
import numpy as np
import ml_dtypes
import concourse.bass as bass
import concourse.mybir as mybir
from concourse.bass_utils import run_bass_kernel_spmd

F32 = mybir.dt.float32
BF16 = mybir.dt.bfloat16
AF = mybir.ActivationFunctionType
ALU = mybir.AluOpType

S = 8192
T = 2052
NH = 8
EPS = 1e-6
SCALE = 96.0 ** -0.5
NFF = 22
TILES = [(0, 512), (512, 512), (1024, 512), (1536, 512), (2048, 4)]
FT = [(1 + 510 * i, 512) for i in range(4)] + [(2041, 10)]

V_GMIX, V_GFFN, V_GPLE, V_GFIN, V_GQ, V_GKV, V_CONVW, V_FFNW, V_FFNB, V_HM, NV = (
    0, 8, 16, 24, 32, 34, 35, 47, 179, 223, 225)

ENGS = ("pe", "act", "dve", "pool", "sp")


class Prog:
    def __init__(self):
        self.ops = {e: [] for e in ENGS}
        self.cnt = {}
        self.seen = {e: {} for e in ENGS}
        self.res = {}
        self.dirty = set()

    def _need(self, eng, tok, waits):
        sk, val = tok
        if sk == "pe" and eng == "pe":
            return
        if self.seen[eng].get(sk, 0) >= val:
            return
        self.seen[eng][sk] = val
        waits.append(tok)

    def add(self, eng, fn, reads=(), writes=(), dma=None):
        waits = []
        for k in reads:
            r = self.res.get(k)
            if r and r[0]:
                self._need(eng, r[0], waits)
        for k in writes:
            r = self.res.get(k)
            if r:
                if r[0]:
                    self._need(eng, r[0], waits)
                for tk in r[1]:
                    self._need(eng, tk, waits)
        if dma is not None:
            sk, amt = ("d", dma), 16
        else:
            sk, amt = eng, 1
        val = self.cnt.get(sk, 0) + amt
        self.cnt[sk] = val
        self.dirty.add(sk)
        tok = (sk, val)
        for k in reads:
            self.res.setdefault(k, [None, []])[1].append(tok)
        for k in writes:
            self.res[k] = [tok, []]
        self.ops[eng].append((waits, fn, (sk, amt)))
        return tok

    def barrier(self):
        for e in ENGS:
            waits = []
            for sk in sorted(self.dirty, key=str):
                if sk == e:
                    continue
                self._need(e, (sk, self.cnt[sk]), waits)
            self.ops[e].append((waits, None, None))
        self.res = {}
        self.dirty = set()


def build_program(dbg=None):
    nc = bass.Bass("TRN2", target_bir_lowering=False)

    def din(name, shape):
        return nc.dram_tensor(name, shape, F32, kind="ExternalInput").ap()

    xT_full = din("xT_full", [128, 8, S])
    xT_own = din("xT_own", [128, 8, T])
    pT_own = din("pT_own", [128, 2, 2048])
    w_in = din("w_in", [128, 8, 1952])
    w_uq = din("w_uq", [128, 2, 768])
    w_ukv = din("w_ukv", [128, 1024])
    w_o = din("w_o", [128, 8, 1024])
    w_up = din("w_up", [NFF, 128, 8, 256])
    w_down = din("w_down", [NFF, 128, 1024])
    w_gate = din("w_gate", [128, 8, 1024])
    w_proj = din("w_proj", [128, 2, 1024])
    vecs_d = din("vecs", [128, NV])
    cosk = din("cosk", [32, S])
    sink = din("sink", [32, S])
    cosq = din("cosq", [32, T])
    sinq = din("sinq", [32, T])
    outT = nc.dram_tensor("outT", [128, 8, 2048], F32, kind="ExternalOutput").ap()
    dbg_out = None
    if dbg is not None:
        dbg_out = nc.dram_tensor("dbg", [128, dbg[1]], F32, kind="ExternalOutput").ap()

    P = Prog()
    AW = 52000
    dma_ids = {}

    def dsem(name):
        if name not in dma_ids:
            dma_ids[name] = len(dma_ids)
        return dma_ids[name]

    with nc.sbuf_tensor("arena", [128, AW], F32) as arena_t, \
            nc.psum_tensor("ps", [128, 4096], F32) as ps_t:
        arena = arena_t[:]
        ps = ps_t[:]

        def bank(i, n=512, p0=0, p1=128, nb=1):
            return ps[p0:p1, i * 512:i * 512 + (n if nb == 1 else nb * 512)]

        class Zone:
            def __init__(self, base, size):
                self.base, self.size, self.off = base, size, 0

            def reset(self):
                self.off = 0

            def f32(self, n):
                n2 = (n + 1) // 2 * 2
                assert self.off + n2 <= self.size, ("arena overflow", self.off, n2, self.size)
                ap = arena[:, self.base + self.off:self.base + self.off + n]
                self.off += n2
                return ap

            def bf16(self, n):
                w = (n + 3) // 4 * 2
                assert self.off + w <= self.size, ("arena overflow", self.off, w, self.size)
                ap = arena[:, self.base + self.off:self.base + self.off + w].bitcast(BF16)[:, 0:n]
                self.off += w
                return ap

        Z0 = Zone(0, 2400)
        Z1 = Zone(2400, 10300)
        Z2 = Zone(12700, 8208)
        Z3 = Zone(20908, AW - 20908)

        vec = Z0.f32(NV)
        ones = Z0.bf16(128)
        wuq = Z0.bf16(2 * 768).rearrange("p (c n) -> p c n", c=2)
        wuqr = Z0.bf16(2 * 8 * 96).rearrange("p (c h n) -> p c h n", c=2, h=8)
        wukv = Z0.bf16(1024)
        kvn = Z1.bf16(S)
        krope = Z1.bf16(S)
        qn = Z1.bf16(2 * T).rearrange("p (c n) -> p c n", c=2)
        yT = Z2.bf16(8 * T).rearrange("p (c n) -> p c n", c=8)

        def V(c0, n=1):
            return vec[:, c0:c0 + n]

        P.add("sp", lambda e: e.dma_start(out=vec, in_=vecs_d[:, :]), writes=["vec"], dma=dsem("vec"))
        P.add("pool", lambda e: e.dma_start(out=wuq, in_=w_uq[:, :, :]), writes=["wuq"], dma=dsem("wuq"))
        P.add("pool", lambda e: e.dma_start(out=wukv, in_=w_ukv[:, :]), writes=["wukv"], dma=dsem("wukv"))
        P.add("dve", lambda e: e.memset(ones, 1.0), writes=["ones"])
        P.add("dve", lambda e: e.memset(wuqr, 0.0), writes=["wuqr"])
        wuq4 = wuq.rearrange("p c (h n) -> p c h n", h=8)
        P.add("dve", lambda e: e.tensor_scalar(out=wuqr[:, :, :, 64:80], in0=wuq4[:, :, :, 80:96], scalar1=-1.0,
                                               scalar2=None, op0=ALU.mult), reads=["wuq"], writes=["wuqr"])
        P.add("dve", lambda e: e.tensor_copy(out=wuqr[:, :, :, 80:96], in_=wuq4[:, :, :, 64:80]),
              reads=["wuq"], writes=["wuqr"])
        P.add("pool", lambda e: e.memset(yT[:, :, 0:1], 0.0), writes=["yT0"])
        P.add("pool", lambda e: e.memset(yT[:, :, T - 1:T], 0.0), writes=["yT0"])

        Z3.reset()
        wA = Z3.bf16(8 * 320).rearrange("p (c n) -> p c n", c=8)
        A_xt = [Z3.f32(8 * 512).rearrange("p (c n) -> p c n", c=8) for _ in range(2)]
        A_sq = [Z3.bf16(8 * 512).rearrange("p (c n) -> p c n", c=8) for _ in range(2)]
        A_xg = [Z3.bf16(8 * 512).rearrange("p (c n) -> p c n", c=8) for _ in range(2)]
        A_t = [{k: Z3.f32(512) for k in ("sx", "rstd", "kvl", "sk", "rk", "ca", "cb", "cs", "sn")} for _ in range(2)]
        A_sqk = [Z3.bf16(512) for _ in range(2)]

        P.add("pool", lambda e: e.dma_start(out=wA[:, :, 0:128], in_=w_in[:, :, 1792:1920]), writes=["wA"], dma=dsem("wA"))
        P.add("pool", lambda e: e.dma_start(out=wA[:, :, 192:224], in_=w_in[:, :, 1920:1952]), writes=["wA1"], dma=dsem("wA"))
        P.add("dve", lambda e: e.memset(wA[:, :, 128:192], 0.0), writes=["wAz"])
        P.add("dve", lambda e: e.memset(wA[:, :, 224:288], 0.0), writes=["wAz"])
        P.add("dve", lambda e: e.tensor_scalar(out=wA[:, :, 288:304], in0=wA[:, :, 208:224], scalar1=-1.0, scalar2=None,
                                               op0=ALU.mult), reads=["wA1"], writes=["wAr"])
        P.add("dve", lambda e: e.tensor_copy(out=wA[:, :, 304:320], in_=wA[:, :, 192:208]), reads=["wA1"], writes=["wAr2"])
        gmix_b = V(V_GMIX, 8).unsqueeze(2).to_broadcast([128, 8, 512])

        def A_front(i):
            s = i % 2
            c0 = i * 512
            tt = A_t[s]
            P.add("sp", lambda e, s=s, c0=c0: e.dma_start(out=A_xt[s], in_=xT_full[:, :, c0:c0 + 512]),
                  writes=[("Axt", s)], dma=dsem(("Axt", s)))
            P.add("sp", lambda e, tt=tt, c0=c0: e.dma_start(out=tt["cs"][64:96, :], in_=cosk[:, c0:c0 + 512]),
                  writes=[("Acs", s)], dma=dsem(("Acs", s)))
            P.add("sp", lambda e, tt=tt, c0=c0: e.dma_start(out=tt["sn"][64:96, :], in_=sink[:, c0:c0 + 512]),
                  writes=[("Asn", s)], dma=dsem(("Asn", s)))
            P.add("act", lambda e, s=s: e.activation(out=A_sq[s], in_=A_xt[s], func=AF.Square),
                  reads=[("Axt", s)], writes=[("Asq", s)])
            P.add("dve", lambda e, s=s: e.tensor_tensor(out=A_xg[s], in0=A_xt[s], in1=gmix_b, op=ALU.mult),
                  reads=[("Axt", s), "vec"], writes=[("Axg", s)])

            def pe1(e, s=s):
                ins = None
                for c in range(8):
                    ins = e.matmul(bank(4 * s), lhsT=ones, rhs=A_sq[s][:, c, :], start=(c == 0), stop=(c == 7))
                return ins
            P.add("pe", pe1, reads=[("Asq", s), "ones"], writes=[("ps", 4 * s)])

            def pe2(e, s=s):
                ins = None
                for c in range(8):
                    ins = e.matmul(bank(4 * s + 1), lhsT=wA[:, c, 0:128], rhs=A_xg[s][:, c, :], start=(c == 0), stop=(c == 7))
                for c in range(8):
                    ins = e.matmul(bank(4 * s + 2, p1=96), lhsT=wA[:, c, 128:224], rhs=A_xg[s][:, c, :], start=(c == 0), stop=(c == 7))
                for c in range(8):
                    ins = e.matmul(bank(4 * s + 3, p1=96), lhsT=wA[:, c, 224:320], rhs=A_xg[s][:, c, :], start=(c == 0), stop=(c == 7))
                return ins
            P.add("pe", pe2, reads=[("Axg", s), "wA", "wA1", "wAz", "wAr", "wAr2"],
                  writes=[("ps", 4 * s + 1), ("ps", 4 * s + 2), ("ps", 4 * s + 3)])

        def A_back(i):
            s = i % 2
            c0 = i * 512
            tt = A_t[s]
            P.add("act", lambda e, s=s, tt=tt: e.activation(out=tt["sx"], in_=bank(4 * s), func=AF.Sqrt, scale=1.0 / 1024, bias=V_EPS),
                  reads=[("ps", 4 * s), "eps"], writes=[("Asx", s)])
            P.add("dve", lambda e, tt=tt: e.reciprocal(out=tt["rstd"], in_=tt["sx"]), reads=[("Asx", s)], writes=[("Arstd", s)])
            P.add("dve", lambda e, s=s, tt=tt: e.tensor_tensor(out=tt["kvl"], in0=bank(4 * s + 1), in1=tt["rstd"], op=ALU.mult),
                  reads=[("ps", 4 * s + 1), ("Arstd", s)], writes=[("Akvl", s)])
            P.add("act", lambda e, s=s, tt=tt: e.activation(out=A_sqk[s], in_=tt["kvl"], func=AF.Square),
                  reads=[("Akvl", s)], writes=[("Asqk", s)])
            P.add("pe", lambda e, s=s: e.matmul(bank(4 * s), lhsT=ones, rhs=A_sqk[s], start=True, stop=True),
                  reads=[("Asqk", s), "ones"], writes=[("ps", 4 * s)])
            P.add("act", lambda e, s=s, tt=tt: e.activation(out=tt["sk"], in_=bank(4 * s), func=AF.Sqrt, scale=1.0 / 128, bias=V_EPS),
                  reads=[("ps", 4 * s), "eps"], writes=[("Ask", s)])
            P.add("dve", lambda e, tt=tt: e.reciprocal(out=tt["rk"], in_=tt["sk"]), reads=[("Ask", s)], writes=[("Ark", s)])
            P.add("dve", lambda e, tt=tt, c0=c0: e.scalar_tensor_tensor(out=kvn[:, c0:c0 + 512], in0=tt["kvl"], scalar=V(V_GKV),
                                                                         in1=tt["rk"], op0=ALU.mult, op1=ALU.mult),
                  reads=[("Akvl", s), ("Ark", s), "vec"], writes=[("kvn", i)])
            P.add("dve", lambda e, s=s, tt=tt: e.tensor_tensor(out=tt["ca"][64:96, :], in0=bank(4 * s + 2, p0=64, p1=96),
                                                               in1=tt["rstd"][64:96, :], op=ALU.mult),
                  reads=[("ps", 4 * s + 2), ("Arstd", s)], writes=[("Aca", s)])
            P.add("dve", lambda e, s=s, tt=tt: e.tensor_tensor(out=tt["cb"][64:96, :], in0=bank(4 * s + 3, p0=64, p1=96),
                                                               in1=tt["rstd"][64:96, :], op=ALU.mult),
                  reads=[("ps", 4 * s + 3), ("Arstd", s)], writes=[("Acb", s)])
            P.add("dve", lambda e, tt=tt: e.tensor_tensor(out=tt["ca"][64:96, :], in0=tt["ca"][64:96, :], in1=tt["cs"][64:96, :], op=ALU.mult),
                  reads=[("Aca", s), ("Acs", s)], writes=[("Aca", s)])
            P.add("dve", lambda e, tt=tt: e.tensor_tensor(out=tt["cb"][64:96, :], in0=tt["cb"][64:96, :], in1=tt["sn"][64:96, :], op=ALU.mult),
                  reads=[("Acb", s), ("Asn", s)], writes=[("Acb", s)])
            P.add("dve", lambda e, tt=tt, c0=c0: e.tensor_tensor(out=krope[64:96, c0:c0 + 512], in0=tt["ca"][64:96, :], in1=tt["cb"][64:96, :], op=ALU.add),
                  reads=[("Aca", s), ("Acb", s)], writes=[("krope", i)])

        for i in range(17):
            if i < 16:
                A_front(i)
            if i >= 1:
                A_back(i - 1)

        P.barrier()
        if dbg and dbg[0] == "A":
            P.add("sp", lambda e: e.dma_start(out=dbg_out[:, 0:S].bitcast(BF16)[:, 0:S], in_=kvn), dma=dsem("dbg"))
            P.add("sp", lambda e: e.dma_start(out=dbg_out[:, S:2 * S].bitcast(BF16)[:, 0:S], in_=krope), dma=dsem("dbg"))
            P.barrier()
            return finish(nc, P, dma_ids)

        Z3.reset()
        xgT = Z3.bf16(8 * T).rearrange("p (c n) -> p c n", c=8)
        rstd1 = Z3.f32(T)
        mark = Z3.off
        B_xt = [Z3.f32(8 * 512).rearrange("p (c n) -> p c n", c=8) for _ in range(2)]
        B_sq = [Z3.bf16(8 * 512).rearrange("p (c n) -> p c n", c=8) for _ in range(2)]
        B_sx = [Z3.f32(512) for _ in range(2)]
        for ti, (c0, n) in enumerate(TILES):
            s = ti % 2
            P.add("sp", lambda e, s=s, c0=c0, n=n: e.dma_start(out=B_xt[s][:, :, 0:n], in_=xT_own[:, :, c0:c0 + n]),
                  writes=[("Bxt", s)], dma=dsem(("Bxt", s)))
            P.add("act", lambda e, s=s, n=n: e.activation(out=B_sq[s][:, :, 0:n], in_=B_xt[s][:, :, 0:n], func=AF.Square),
                  reads=[("Bxt", s)], writes=[("Bsq", s)])
            P.add("dve", lambda e, s=s, c0=c0, n=n: e.tensor_tensor(out=xgT[:, :, c0:c0 + n], in0=B_xt[s][:, :, 0:n],
                                                                    in1=V(V_GMIX, 8).unsqueeze(2).to_broadcast([128, 8, n]), op=ALU.mult),
                  reads=[("Bxt", s), "vec"], writes=[("xgT", ti)])

            def pe1(e, s=s, n=n):
                ins = None
                for c in range(8):
                    ins = e.matmul(bank(s, n), lhsT=ones, rhs=B_sq[s][:, c, 0:n], start=(c == 0), stop=(c == 7))
                return ins
            P.add("pe", pe1, reads=[("Bsq", s), "ones"], writes=[("ps", s)])
            P.add("act", lambda e, s=s, n=n: e.activation(out=B_sx[s][:, 0:n], in_=bank(s, n), func=AF.Sqrt, scale=1.0 / 1024, bias=V_EPS),
                  reads=[("ps", s), "eps"], writes=[("Bsx", s)])
            P.add("dve", lambda e, s=s, c0=c0, n=n: e.reciprocal(out=rstd1[:, c0:c0 + n], in_=B_sx[s][:, 0:n]),
                  reads=[("Bsx", s)], writes=[("rstd1", ti)])
        P.barrier()
        Z3.off = mark
        wS = [Z3.bf16(8 * 384).rearrange("p (c n) -> p c n", c=8) for _ in range(2)]
        wQ = Z3.bf16(8 * 256).rearrange("p (c n) -> p c n", c=8)
        B_xs = [Z3.f32(512) for _ in range(2)]
        B_t1 = [Z3.f32(512) for _ in range(2)]
        B_u = Z3.f32(T)
        B_bg = Z3.f32(T)
        B_c = Z3.f32(T)
        ql = Z3.f32(2 * T).rearrange("p (c n) -> p c n", c=2)
        B_sqq = [Z3.bf16(512) for _ in range(2)]
        B_sxq = [Z3.f32(512) for _ in range(2)]
        B_rq = [Z3.f32(512) for _ in range(2)]
        P.add("pool", lambda e: e.dma_start(out=wQ, in_=w_in[:, :, 1536:1792]), writes=["wQ"], dma=dsem("wQ"))
        pb = [0]

        def nextbank():
            pb[0] = (pb[0] + 1) % 8
            return pb[0]

        for k in range(4):
            ws = k % 2
            for j, src in enumerate((k * 128, 1024 + k * 128, 512 + k * 128)):
                P.add("pool", lambda e, ws=ws, j=j, src=src: e.dma_start(out=wS[ws][:, :, j * 128:(j + 1) * 128], in_=w_in[:, :, src:src + 128]),
                      writes=[("wS", ws, j)], dma=dsem(("wS", ws)))
            for ti, (c0, n) in enumerate(TILES):
                s = ti % 2
                bx, bc, bb = nextbank(), nextbank(), nextbank()

                def pe3(e, ws=ws, c0=c0, n=n, bx=bx, bc=bc, bb=bb):
                    ins = None
                    for j, b in enumerate((bx, bc, bb)):
                        for c in range(8):
                            ins = e.matmul(bank(b, n), lhsT=wS[ws][:, c, j * 128:(j + 1) * 128], rhs=xgT[:, c, c0:c0 + n],
                                           start=(c == 0), stop=(c == 7))
                    return ins
                P.add("pe", pe3, reads=[("wS", ws, 0), ("wS", ws, 1), ("wS", ws, 2), ("xgT", ti)], writes=[("ps", bx), ("ps", bc), ("ps", bb)])
                P.add("dve", lambda e, s=s, c0=c0, n=n, bx=bx: e.tensor_tensor(out=B_xs[s][:, 0:n], in0=bank(bx, n), in1=rstd1[:, c0:c0 + n], op=ALU.mult),
                      reads=[("ps", bx), ("rstd1", ti)], writes=[("Bxs", s)])
                P.add("dve", lambda e, s=s, c0=c0, n=n, bc=bc: e.tensor_tensor(out=B_t1[s][:, 0:n], in0=bank(bc, n), in1=rstd1[:, c0:c0 + n], op=ALU.mult),
                      reads=[("ps", bc), ("rstd1", ti)], writes=[("Bt1", s)])
                P.add("pool", lambda e, s=s, c0=c0, n=n: e.tensor_tensor(out=B_u[:, c0:c0 + n], in0=B_xs[s][:, 0:n], in1=B_t1[s][:, 0:n], op=ALU.mult),
                      reads=[("Bxs", s), ("Bt1", s)], writes=[("Bu", ti)])
                P.add("dve", lambda e, c0=c0, n=n, bb=bb: e.tensor_tensor(out=B_bg[:, c0:c0 + n], in0=bank(bb, n), in1=rstd1[:, c0:c0 + n], op=ALU.mult),
                      reads=[("ps", bb), ("rstd1", ti)], writes=[("Bbg", ti)])
            allu = [("Bu", ti) for ti in range(5)]
            allbg = [("Bbg", ti) for ti in range(5)]
            cw = V_CONVW + 3 * k
            P.add("pool", lambda e, cw=cw: e.tensor_scalar(out=B_c[:, 1:T - 1], in0=B_u[:, 1:T - 1], scalar1=V(cw + 1), scalar2=None, op0=ALU.mult),
                  reads=allu + ["vec"], writes=["Bc"])
            P.add("dve", lambda e, cw=cw: e.scalar_tensor_tensor(out=B_c[:, 1:T - 1], in0=B_u[:, 0:T - 2], scalar=V(cw), in1=B_c[:, 1:T - 1],
                                                                  op0=ALU.mult, op1=ALU.add), reads=allu + ["Bc"], writes=["Bc"])
            P.add("dve", lambda e, cw=cw: e.scalar_tensor_tensor(out=B_c[:, 1:T - 1], in0=B_u[:, 2:T], scalar=V(cw + 2), in1=B_c[:, 1:T - 1],
                                                                  op0=ALU.mult, op1=ALU.add), reads=allu + ["Bc"], writes=["Bc"])
            P.add("pool", lambda e, k=k: e.tensor_tensor(out=yT[:, k, 1:T - 1], in0=B_bg[:, 1:T - 1], in1=B_c[:, 1:T - 1], op=ALU.mult),
                  reads=allbg + ["Bc"], writes=[("yT", k)])
        for ti, (c0, n) in enumerate(TILES):
            s = ti % 2
            bq = nextbank()
            for kc in range(2):
                b = nextbank()

                def pe4(e, kc=kc, c0=c0, n=n, b=b):
                    ins = None
                    for c in range(8):
                        ins = e.matmul(bank(b, n), lhsT=wQ[:, c, kc * 128:(kc + 1) * 128], rhs=xgT[:, c, c0:c0 + n], start=(c == 0), stop=(c == 7))
                    return ins
                P.add("pe", pe4, reads=["wQ", ("xgT", ti)], writes=[("ps", b)])
                P.add("dve", lambda e, kc=kc, c0=c0, n=n, b=b: e.tensor_tensor(out=ql[:, kc, c0:c0 + n], in0=bank(b, n), in1=rstd1[:, c0:c0 + n], op=ALU.mult),
                      reads=[("ps", b), ("rstd1", ti)], writes=[("ql", ti, kc)])
                P.add("act", lambda e, s=s, kc=kc, c0=c0, n=n: e.activation(out=B_sqq[s][:, 0:n], in_=ql[:, kc, c0:c0 + n], func=AF.Square),
                      reads=[("ql", ti, kc)], writes=[("Bsqq", s)])
                P.add("pe", lambda e, s=s, kc=kc, n=n, bq=bq: e.matmul(bank(bq, n), lhsT=ones, rhs=B_sqq[s][:, 0:n], start=(kc == 0), stop=(kc == 1)),
                      reads=[("Bsqq", s), "ones"], writes=[("ps", bq)])
            P.add("act", lambda e, s=s, n=n, bq=bq: e.activation(out=B_sxq[s][:, 0:n], in_=bank(bq, n), func=AF.Sqrt, scale=1.0 / 256, bias=V_EPS),
                  reads=[("ps", bq), "eps"], writes=[("Bsxq", s)])
            P.add("dve", lambda e, s=s, n=n: e.reciprocal(out=B_rq[s][:, 0:n], in_=B_sxq[s][:, 0:n]), reads=[("Bsxq", s)], writes=[("Brq", s)])
            for kc in range(2):
                P.add("dve", lambda e, s=s, kc=kc, c0=c0, n=n: e.scalar_tensor_tensor(out=qn[:, kc, c0:c0 + n], in0=ql[:, kc, c0:c0 + n], scalar=V(V_GQ + kc),
                                                                                      in1=B_rq[s][:, 0:n], op0=ALU.mult, op1=ALU.mult),
                      reads=[("ql", ti, kc), ("Brq", s), "vec"], writes=[("qn", ti)])
        P.barrier()
        if dbg and dbg[0] == "B":
            P.add("sp", lambda e: e.dma_start(out=dbg_out[:, 0:4 * T].bitcast(BF16)[:, 0:8 * T].rearrange("p (c n) -> p c n", c=8), in_=yT), dma=dsem("dbg"))
            P.add("sp", lambda e: e.dma_start(out=dbg_out[:, 4 * T:5 * T].bitcast(BF16)[:, 0:2 * T].rearrange("p (c n) -> p c n", c=2), in_=qn), dma=dsem("dbg"))
            P.barrier()
            return finish(nc, P, dma_ids)

        Z3.reset()
        Kh = [Z3.bf16(S) for _ in range(2)]
        Vh = [Z3.bf16(64 * 128).rearrange("p (k d) -> p k d", d=128) for _ in range(2)]
        Qh = [Z3.bf16(T) for _ in range(2)]
        Pb = [Z3.bf16(1024) for _ in range(4)]
        cq = Z3.f32(T)
        sq_ = Z3.f32(T)
        R_t1 = [Z3.f32(512) for _ in range(2)]
        R_t2 = [Z3.f32(512) for _ in range(2)]
        rec = Z3.f32(1024)
        Ph = Z3.bf16(128)
        rech = Z3.f32(2)
        P.add("sp", lambda e: e.dma_start(out=cq[64:96, :], in_=cosq[:, :]), writes=["cq"], dma=dsem("cq"))
        P.add("sp", lambda e: e.dma_start(out=sq_[64:96, :], in_=sinq[:, :]), writes=["sq"], dma=dsem("sq"))
        for s in range(2):
            P.add("pool", lambda e, s=s: e.memset(Vh[s][:, :, 64:128], 1.0), writes=[("Vones", s)])
        bb = [0]

        def bbank():
            bb[0] = 1 - bb[0]
            return 6 + bb[0]

        def build_steps(h):
            s = h % 2
            steps = []
            steps.append(lambda: P.add("sp", lambda e: e.dma_start(out=Kh[s][64:96, :], in_=krope[64:96, :]),
                                       reads=[("krope", i) for i in range(16)] if h < 2 else [], writes=[("Khr", s)], dma=dsem(("Khr", s))))
            for t in range(16):
                def st(t=t):
                    b = bbank()
                    P.add("pe", lambda e: e.matmul(bank(b, p1=64), lhsT=wukv[:, h * 128:h * 128 + 64], rhs=kvn[:, t * 512:(t + 1) * 512], start=True, stop=True),
                          reads=["wukv"] + ([("kvn", t)] if h < 2 else []), writes=[("ps", b)])
                    P.add("dve", lambda e: e.tensor_copy(out=Kh[s][0:64, t * 512:(t + 1) * 512], in_=bank(b, p1=64)),
                          reads=[("ps", b)], writes=[("Kh", s, t)])
                steps.append(st)
            for g in range(8):
                def st(g=g):
                    b = bbank()

                    def pev(e):
                        ins = None
                        for j in range(8):
                            kt = 8 * g + j
                            ins = e.matmul(bank(b)[:, j * 64:(j + 1) * 64], lhsT=kvn[:, kt * 128:(kt + 1) * 128], rhs=wukv[:, h * 128 + 64:h * 128 + 128],
                                           start=True, stop=True)
                        return ins
                    P.add("pe", pev, reads=["wukv"] + ([("kvn", t) for t in range(16)] if h < 2 else []), writes=[("ps", b)])
                    P.add("dve", lambda e: e.tensor_copy(out=Vh[s][:, 8 * g:8 * g + 8, 0:64], in_=bank(b).rearrange("p (j d) -> p j d", d=64)),
                          reads=[("ps", b)], writes=[("Vh", s, g)])
                steps.append(st)
            for ti, (c0, n) in enumerate(TILES):
                def st(ti=ti, c0=c0, n=n):
                    ba, bb_ = bbank(), bbank()
                    r = ti % 2

                    def peq(e):
                        ins = None
                        for kc in range(2):
                            ins = e.matmul(bank(ba, n, p1=96), lhsT=wuq[:, kc, h * 96:(h + 1) * 96], rhs=qn[:, kc, c0:c0 + n], start=(kc == 0), stop=(kc == 1))
                        for kc in range(2):
                            ins = e.matmul(bank(bb_, n, p1=96), lhsT=wuqr[:, kc, h, :], rhs=qn[:, kc, c0:c0 + n], start=(kc == 0), stop=(kc == 1))
                        return ins
                    P.add("pe", peq, reads=["wuq", "wuqr"] + ([("qn", ti)] if h < 2 else []), writes=[("ps", ba), ("ps", bb_)])
                    P.add("dve", lambda e: e.tensor_copy(out=Qh[s][0:64, c0:c0 + n], in_=bank(ba, n, p1=64)), reads=[("ps", ba)], writes=[("Qh", s, ti)])
                    P.add("dve", lambda e: e.tensor_tensor(out=R_t1[r][64:96, 0:n], in0=bank(ba, n, p0=64, p1=96), in1=cq[64:96, c0:c0 + n], op=ALU.mult),
                          reads=[("ps", ba), "cq"], writes=[("Rt1", r)])
                    P.add("dve", lambda e: e.tensor_tensor(out=R_t2[r][64:96, 0:n], in0=bank(bb_, n, p0=64, p1=96), in1=sq_[64:96, c0:c0 + n], op=ALU.mult),
                          reads=[("ps", bb_), "sq"], writes=[("Rt2", r)])
                    P.add("dve", lambda e: e.tensor_tensor(out=Qh[s][64:96, c0:c0 + n], in0=R_t1[r][64:96, 0:n], in1=R_t2[r][64:96, 0:n], op=ALU.add),
                          reads=[("Rt1", r), ("Rt2", r)], writes=[("Qh", s, ti)])
                steps.append(st)
            return steps

        def khv_reads(s):
            return ([("Khr", s)] + [("Kh", s, t) for t in range(16)] + [("Qh", s, ti) for ti in range(5)])

        def v_reads(s):
            return [("Vh", s, g) for g in range(8)] + [("Vones", s)]

        for st in build_steps(0):
            st()
        pcount = [0]
        for h in range(NH):
            s = h % 2
            pending = build_steps(h + 1) if h + 1 < NH else []
            kreads, vreads = khv_reads(s), v_reads(s)
            dst_p0 = (h % 2) * 64
            ych = 4 + h // 2
            bS = bbank()

            def pehs(e, s=s, bS=bS):
                ins = None
                for kt in range(64):
                    ins = e.matmul(bank(bS)[:, 2 * kt:2 * kt + 2], lhsT=Kh[s][0:96, kt * 128:(kt + 1) * 128], rhs=Qh[s][0:96, 1:T - 1:T - 3], start=True, stop=True)
                return ins
            P.add("pe", pehs, reads=kreads, writes=[("ps", bS)])
            P.add("act", lambda e, bS=bS: e.activation(out=Ph, in_=bank(bS, 128), func=AF.Exp, scale=SCALE), reads=[("ps", bS)], writes=["Ph"])
            bO = bbank()

            def peho(e, s=s, bO=bO):
                ins = None
                for kt in range(64):
                    ins = e.matmul(bank(bO, 2), lhsT=Vh[s][:, kt, :], rhs=Ph[:, 2 * kt:2 * kt + 2], start=(kt == 0), stop=(kt == 63))
                return ins
            P.add("pe", peho, reads=vreads + ["Ph"], writes=[("ps", bO)])
            P.add("dve", lambda e, bO=bO: e.reciprocal(out=rech[0:64, :], in_=bank(bO, 2, p0=64, p1=128)), reads=[("ps", bO)], writes=["rech"])
            P.add("dve", lambda e, bO=bO, dst_p0=dst_p0, ych=ych: e.tensor_tensor(out=yT[dst_p0:dst_p0 + 64, ych, 1:T - 1:T - 3], in0=bank(bO, 2, p1=64),
                                                                                  in1=rech[0:64, :], op=ALU.mult),
                  reads=[("ps", bO), "rech"], writes=[("yTh", h)])
            seq = [(qh, kt) for qh in range(2) for kt in range(64)]
            nsteps = len(seq)

            def issue_S(idx, s=s):
                qh, kt = seq[idx]
                sb = idx % 2
                q0 = 2 + 1024 * qh

                def pes(e):
                    ins = None
                    for j in range(2):
                        ins = e.matmul(bank(2 * sb + j), lhsT=Kh[s][0:96, kt * 128:(kt + 1) * 128], rhs=Qh[s][0:96, q0 + 512 * j:q0 + 512 * (j + 1)],
                                       start=True, stop=True)
                    return ins
                P.add("pe", pes, reads=kreads, writes=[("ps", 2 * sb), ("ps", 2 * sb + 1)])

            def issue_EP(idx, s=s, dst_p0=dst_p0, ych=ych):
                qh, kt = seq[idx]
                sb = idx % 2
                q0 = 2 + 1024 * qh
                pi = pcount[0] % 4
                pcount[0] += 1
                P.add("act", lambda e: e.activation(out=Pb[pi], in_=bank(2 * sb, nb=2), func=AF.Exp, scale=SCALE),
                      reads=[("ps", 2 * sb), ("ps", 2 * sb + 1)], writes=[("Pb", pi)])

                def pepv(e):
                    ins = None
                    for j in range(2):
                        ins = e.matmul(bank(4 + j), lhsT=Vh[s][:, kt, :], rhs=Pb[pi][:, 512 * j:512 * (j + 1)], start=(kt == 0), stop=(kt == 63))
                    return ins
                P.add("pe", pepv, reads=vreads + [("Pb", pi)], writes=[("ps", 4), ("ps", 5)])
                if kt == 63:
                    P.add("dve", lambda e: e.reciprocal(out=rec[0:64, :], in_=bank(4, nb=2, p0=64, p1=128)), reads=[("ps", 4), ("ps", 5)], writes=["rec"])
                    P.add("dve", lambda e: e.tensor_tensor(out=yT[dst_p0:dst_p0 + 64, ych, q0:q0 + 1024], in0=bank(4, nb=2, p1=64), in1=rec[0:64, :], op=ALU.mult),
                          reads=[("ps", 4), ("ps", 5), "rec"], writes=[("yTm", h, qh)])

            issue_S(0)
            for idx in range(nsteps):
                if idx + 1 < nsteps:
                    issue_S(idx + 1)
                issue_EP(idx)
                if pending and idx % 3 == 2:
                    pending.pop(0)()
            while pending:
                pending.pop(0)()
        P.barrier()
        if dbg and dbg[0] == "C":
            P.add("sp", lambda e: e.dma_start(out=dbg_out[:, 0:4 * T].bitcast(BF16)[:, 0:8 * T].rearrange("p (c n) -> p c n", c=8), in_=yT), dma=dsem("dbg"))
            P.barrier()
            return finish(nc, P, dma_ids)


        Z3.reset()
        x1 = Z3.f32(8 * T).rearrange("p (c n) -> p c n", c=8)
        hf = Z3.bf16(8 * T).rearrange("p (c n) -> p c n", c=8)
        wo_raw = Z3.bf16(8 * 1024)
        wo = wo_raw.rearrange("p (c n) -> p c n", c=8)
        stg = [Z3.f32(4 * 512).rearrange("p (c n) -> p c n", c=4) for _ in range(1)]
        Z1.reset()
        xr = [Z1.f32(T) for _ in range(2)]
        rf = Z1.f32(T)
        sqf = [Z1.bf16(8 * 512).rearrange("p (c n) -> p c n", c=8) for _ in range(2)]
        D_sx = [wo_raw.bitcast(F32)[:, 0:512], wo_raw.bitcast(F32)[:, 512:1024]]
        P.add("pool", lambda e: e.dma_start(out=wo, in_=w_o[:, :, :]), writes=["wo"], dma=dsem("wo"))
        for d in range(8):
            r = d % 2
            P.add("sp", lambda e, d=d, r=r: e.dma_start(out=xr[r], in_=xT_own[:, d, :]), writes=[("xr", r)], dma=dsem(("xr", r)))
            for ti, (c0, n) in enumerate(TILES):
                b = nextbank()

                def pewo(e, d=d, c0=c0, n=n, b=b):
                    ins = None
                    for c in range(8):
                        ins = e.matmul(bank(b, n), lhsT=wo[:, c, d * 128:(d + 1) * 128], rhs=yT[:, c, c0:c0 + n], start=(c == 0), stop=(c == 7))
                    return ins
                P.add("pe", pewo, reads=["wo"], writes=[("ps", b)])
                P.add("dve", lambda e, d=d, r=r, c0=c0, n=n, b=b: e.tensor_tensor(out=x1[:, d, c0:c0 + n], in0=bank(b, n), in1=xr[r][:, c0:c0 + n], op=ALU.add),
                      reads=[("ps", b), ("xr", r)], writes=[("x1", d, ti)])

        def rmsnorm_to(src, dst, gcol, tiles, rbuf, tag, sqf, D_sx):
            for ti, (c0, n) in enumerate(tiles):
                s = ti % 2
                bq = nextbank()
                P.add("act", lambda e, s=s, c0=c0, n=n: e.activation(out=sqf[s][:, :, 0:n], in_=src[:, :, c0:c0 + n], func=AF.Square),
                      reads=[(tag, d, "all") for d in range(8)] + [("x1", d, ti) for d in range(8)], writes=[("sqf", s)])

                def pen(e, s=s, n=n, bq=bq):
                    ins = None
                    for c in range(8):
                        ins = e.matmul(bank(bq, n), lhsT=ones, rhs=sqf[s][:, c, 0:n], start=(c == 0), stop=(c == 7))
                    return ins
                P.add("pe", pen, reads=[("sqf", s), "ones"], writes=[("ps", bq)])
                P.add("act", lambda e, s=s, n=n, bq=bq: e.activation(out=D_sx[s][:, 0:n], in_=bank(bq, n), func=AF.Sqrt, scale=1.0 / 1024, bias=V_EPS),
                      reads=[("ps", bq)], writes=[("Dsx", s), "wo"])
                P.add("dve", lambda e, s=s, c0=c0, n=n: e.reciprocal(out=rbuf[:, c0:c0 + n], in_=D_sx[s][:, 0:n]), reads=[("Dsx", s)], writes=[("rbuf", ti)])
                if dst is not None:
                    for d in range(8):
                        P.add("dve", lambda e, d=d, c0=c0, n=n: e.scalar_tensor_tensor(out=dst[:, d, c0:c0 + n], in0=src[:, d, c0:c0 + n], scalar=V(gcol + d),
                                                                                       in1=rbuf[:, c0:c0 + n], op0=ALU.mult, op1=ALU.mult),
                              reads=[("rbuf", ti), (tag, d, "all"), ("x1", d, ti), "vec"], writes=[("hf", d, ti)])

        rmsnorm_to(x1, hf, V_GFFN, TILES, rf, "x1", sqf, D_sx)
        P.add("dve", lambda e: e.tensor_scalar(out=hf[:, :, 1:2], in0=hf[:, :, 1:2], scalar1=V(V_HM), scalar2=None, op0=ALU.mult),
              reads=[("hf", d, 0) for d in range(8)] + ["vec"], writes=[("hf", d, 0) for d in range(8)])
        P.add("dve", lambda e: e.tensor_scalar(out=hf[:, :, T - 2:T - 1], in0=hf[:, :, T - 2:T - 1], scalar1=V(V_HM + 1), scalar2=None, op0=ALU.mult),
              reads=[("hf", d, 4) for d in range(8)] + ["vec"], writes=[("hf", d, 4) for d in range(8)])
        P.barrier()
        if dbg and dbg[0] == "D":
            P.add("sp", lambda e: e.dma_start(out=dbg_out[:, 0:8 * T].rearrange("p (c n) -> p c n", c=8), in_=x1), dma=dsem("dbg"))
            P.barrier()
            return finish(nc, P, dma_ids)

        ZF = Zone(2400, 18508)
        G = 4
        wup = [ZF.bf16(8 * 256).rearrange("p (c n) -> p c n", c=8) for _ in range(3)]
        wdn = [ZF.bf16(1024) for _ in range(8)]
        a0g = [ZF.f32(512) for _ in range(2)]
        a0u = [ZF.f32(512) for _ in range(2)]
        sgb = [ZF.f32(512) for _ in range(2)]
        actT = [[ZF.bf16(2048) for _ in range(G)] for _ in range(2)]
        hf_all = [("hf", d, ti) for d in range(8) for ti in range(5)]

        def load_w(j):
            P.add("pool", lambda e, j=j: e.dma_start(out=wup[j % 3], in_=w_up[j, :, :, :]), writes=[("wup", j % 3)], dma=dsem(("wup", j % 3)))
            P.add("pool", lambda e, j=j: e.dma_start(out=wdn[j % 8], in_=w_down[j, :, :]), writes=[("wdn", j % 8)], dma=dsem(("wdn", j % 8)))

        cnt2 = [0]

        def U(j):
            if j + 2 < NFF:
                load_w(j + 2)
            gs, jj = (j // G) % 2, j % G
            for fi, (c0, n) in enumerate(FT):
                s = cnt2[0] % 2
                cnt2[0] += 1
                bg_, bu_ = nextbank(), nextbank()

                def peu(e, c0=c0, n=n, bg_=bg_, bu_=bu_):
                    ins = None
                    for c in range(8):
                        ins = e.matmul(bank(bg_, n), lhsT=wup[j % 3][:, c, 0:128], rhs=hf[:, c, c0:c0 + n], start=(c == 0), stop=(c == 7))
                    for c in range(8):
                        ins = e.matmul(bank(bu_, n), lhsT=wup[j % 3][:, c, 128:256], rhs=hf[:, c, c0:c0 + n], start=(c == 0), stop=(c == 7))
                    return ins
                P.add("pe", peu, reads=[("wup", j % 3)] + hf_all, writes=[("ps", bg_), ("ps", bu_)])
                wg_, wu_ = V_FFNW + 3 * j, V_FFNW + 3 * (j + NFF)
                P.add("act", lambda e, s=s, n=n, bg_=bg_, wg_=wg_: e.activation(out=a0g[s][:, 0:n], in_=bank(bg_, n), func=AF.Identity, scale=V(wg_ + 1), bias=V(V_FFNB + j)),
                      reads=[("ps", bg_), "vec"], writes=[("a0g", s)])
                P.add("act", lambda e, s=s, n=n, bu_=bu_, wu_=wu_: e.activation(out=a0u[s][:, 0:n], in_=bank(bu_, n), func=AF.Identity, scale=V(wu_ + 1), bias=V(V_FFNB + j + NFF)),
                      reads=[("ps", bu_), "vec"], writes=[("a0u", s)])
                for (buf, key, bk, wv) in ((a0g, "a0g", bg_, wg_), (a0u, "a0u", bu_, wu_)):
                    P.add("dve", lambda e, s=s, n=n, buf=buf, bk=bk, wv=wv: e.scalar_tensor_tensor(out=buf[s][:, 1:n - 1], in0=bank(bk, n)[:, 0:n - 2], scalar=V(wv),
                                                                                                   in1=buf[s][:, 1:n - 1], op0=ALU.mult, op1=ALU.add),
                          reads=[("ps", bk), (key, s), "vec"], writes=[(key, s)])
                    P.add("dve", lambda e, s=s, n=n, buf=buf, bk=bk, wv=wv: e.scalar_tensor_tensor(out=buf[s][:, 1:n - 1], in0=bank(bk, n)[:, 2:n], scalar=V(wv + 2),
                                                                                                   in1=buf[s][:, 1:n - 1], op0=ALU.mult, op1=ALU.add),
                          reads=[("ps", bk), (key, s), "vec"], writes=[(key, s)])
                P.add("act", lambda e, s=s, n=n: e.activation(out=sgb[s][:, 0:n - 2], in_=a0g[s][:, 1:n - 1], func=AF.Silu),
                      reads=[("a0g", s)], writes=[("sgb", s)])
                o0 = c0 - 1
                P.add("dve", lambda e, s=s, n=n, o0=o0, gs=gs, jj=jj: e.tensor_tensor(out=actT[gs][jj][:, o0:o0 + n - 2], in0=sgb[s][:, 0:n - 2], in1=a0u[s][:, 1:n - 1], op=ALU.mult),
                      reads=[("sgb", s), ("a0u", s)], writes=[("actT", gs, jj, fi)])

        def Dn(g):
            gs = g % 2
            js = list(range(g * G, min((g + 1) * G, NFF)))
            areads = [("actT", gs, j % G, fi) for j in js for fi in range(5)] + [("wdn", j % 8) for j in js]
            for d in range(8):
                for q in range(4):
                    b = nextbank()

                    def ped(e, d=d, q=q, b=b):
                        ins = None
                        for ii, j in enumerate(js):
                            ins = e.matmul(bank(b), lhsT=wdn[j % 8][:, d * 128:(d + 1) * 128], rhs=actT[gs][j % G][:, q * 512:(q + 1) * 512],
                                           start=(ii == 0), stop=(ii == len(js) - 1))
                        return ins
                    P.add("pe", ped, reads=areads, writes=[("ps", b)])
                    P.add("dve", lambda e, d=d, q=q, b=b: e.tensor_tensor(out=x1[:, d, 2 + q * 512:2 + (q + 1) * 512], in0=bank(b), in1=x1[:, d, 2 + q * 512:2 + (q + 1) * 512], op=ALU.add),
                          reads=[("ps", b), ("x2", d, q)], writes=[("x2", d, q)])

        load_w(0)
        load_w(1)
        ngroups = (NFF + G - 1) // G
        for j in range(NFF):
            U(j)
            if j % G == 0 and j > 0:
                Dn(j // G - 1)
        Dn(ngroups - 1)
        P.barrier()
        if dbg and dbg[0] == "E":
            P.add("sp", lambda e: e.dma_start(out=dbg_out[:, 0:8 * T].rearrange("p (c n) -> p c n", c=8), in_=x1), dma=dsem("dbg"))
            P.barrier()
            return finish(nc, P, dma_ids)

        ZF.reset()
        wg = ZF.bf16(8 * 1024).rearrange("p (c n) -> p c n", c=8)
        wp = ZF.bf16(2 * 1024).rearrange("p (c n) -> p c n", c=2)
        pTb = ZF.bf16(2 * 2048).rearrange("p (c n) -> p c n", c=2)
        sqf = [ZF.bf16(8 * 512).rearrange("p (c n) -> p c n", c=8) for _ in range(2)]
        D_sx = [ZF.f32(512) for _ in range(2)]
        rp = ZF.f32(T)
        sgm = [ZF.f32(512) for _ in range(2)]
        pj = [ZF.f32(512) for _ in range(2)]
        OWN = [(2 + 512 * q, 512) for q in range(4)]
        P.add("pool", lambda e: e.dma_start(out=wg, in_=w_gate[:, :, :]), writes=["wg"], dma=dsem("wg"))
        P.add("pool", lambda e: e.dma_start(out=wp, in_=w_proj[:, :, :]), writes=["wp"], dma=dsem("wp"))
        P.add("pool", lambda e: e.dma_start(out=pTb, in_=pT_own[:, :, :]), writes=["pTb"], dma=dsem("pTb"))
        rmsnorm_to(x1, hf, V_GPLE, OWN, rp, "x2", sqf, D_sx)
        for d in range(8):
            for q, (c0, n) in enumerate(OWN):
                s = (d * 4 + q) % 2
                b1, b2 = nextbank(), nextbank()

                def peg(e, d=d, q=q, c0=c0, b1=b1, b2=b2):
                    ins = None
                    for c in range(8):
                        ins = e.matmul(bank(b1), lhsT=wg[:, c, d * 128:(d + 1) * 128], rhs=hf[:, c, c0:c0 + 512], start=(c == 0), stop=(c == 7))
                    for kc in range(2):
                        ins = e.matmul(bank(b2), lhsT=wp[:, kc, d * 128:(d + 1) * 128], rhs=pTb[:, kc, q * 512:(q + 1) * 512], start=(kc == 0), stop=(kc == 1))
                    return ins
                P.add("pe", peg, reads=["wg", "wp", "pTb"] + [("hf", dd, q) for dd in range(8)], writes=[("ps", b1), ("ps", b2)])
                P.add("act", lambda e, s=s, b1=b1: e.activation(out=sgm[s], in_=bank(b1), func=AF.Sigmoid), reads=[("ps", b1)], writes=[("sgm", s)])
                P.add("dve", lambda e, s=s, b2=b2: e.tensor_tensor(out=pj[s], in0=bank(b2), in1=sgm[s], op=ALU.mult), reads=[("ps", b2), ("sgm", s)], writes=[("pj", s)])
                P.add("dve", lambda e, s=s, d=d, c0=c0: e.tensor_tensor(out=x1[:, d, c0:c0 + 512], in0=x1[:, d, c0:c0 + 512], in1=pj[s], op=ALU.add),
                      reads=[("pj", s)], writes=[("x3", d, q)])
        P.barrier()
        rmsnorm_to(x1, None, 0, OWN, rp, "x3", sqf, D_sx)
        for q, (c0, n) in enumerate(OWN):
            for hh in range(2):
                for dd in range(4):
                    d = hh * 4 + dd
                    P.add("dve", lambda e, d=d, dd=dd, c0=c0: e.scalar_tensor_tensor(out=stg[0][:, dd, :], in0=x1[:, d, c0:c0 + 512], scalar=V(V_GFIN + d),
                                                                                     in1=rp[:, c0:c0 + 512], op0=ALU.mult, op1=ALU.mult),
                          reads=[("rbuf", q), "vec"], writes=[("stg", dd)])
                P.add("sp", lambda e, q=q, hh=hh: e.dma_start(out=outT[:, hh * 4:(hh + 1) * 4, q * 512:(q + 1) * 512], in_=stg[0]),
                      reads=[("stg", dd) for dd in range(4)], dma=dsem("out"))
        P.barrier()
        return finish(nc, P, dma_ids)


V_EPS = EPS


def finish(nc, P, dma_ids):
    from contextlib import ExitStack
    with ExitStack() as es:
        sems = {}
        for e in ("pe", "act", "dve", "pool"):
            sems[e] = es.enter_context(nc.semaphore("s_" + e))
        for _, i in dma_ids.items():
            sems[("d", i)] = es.enter_context(nc.semaphore("d%d" % i))
        block = es.enter_context(nc.Block())

        def emit(ename):
            def body(h):
                for waits, fn, inc in P.ops[ename]:
                    for sk, val in waits:
                        h.wait_ge(sems[sk], val)
                    if fn is not None:
                        fn(h).then_inc(sems[inc[0]], inc[1])
            return body
        block.tensor(emit("pe"))
        block.scalar(emit("act"))
        block.vector(emit("dve"))
        block.gpsimd(emit("pool"))
        block.sync(emit("sp"))
    return nc


def rope_tables():
    pos = np.arange(S, dtype=np.float32)
    inv_freq = (np.float32(10000.0) ** (-np.arange(0, 32, 2, dtype=np.float32) / np.float32(32))).astype(np.float32)
    ang = (pos[:, None] * inv_freq[None, :]).astype(np.float32)
    ang = np.concatenate([ang, ang], axis=-1)
    return np.cos(ang).astype(np.float32), np.sin(ang).astype(np.float32)


def chunked(w):
    K, N = w.shape
    return np.ascontiguousarray(w.reshape(K // 128, 128, N).transpose(1, 0, 2))


def make_in_maps(x, p, norm_mix_g, w_in, conv_w, q_norm_g, w_uq, kv_norm_g, w_ukv, w_o,
                 norm_ffn_g, w_up, ffn_conv_w, ffn_conv_b, w_down, ple_norm_g,
                 w_ple_gate, w_ple_proj, final_norm_g, cores=range(8)):
    f = lambda a: np.asarray(a, dtype=np.float32)
    x, p = f(x), f(p)
    cos, sin = rope_tables()
    shared = {
        "w_in": chunked(f(w_in)[0]), "w_uq": chunked(f(w_uq)[0]), "w_ukv": np.ascontiguousarray(f(w_ukv)[0]),
        "w_o": chunked(f(w_o)[0]), "w_gate": chunked(f(w_ple_gate)[0]), "w_proj": chunked(f(w_ple_proj)[0]),
        "w_down": np.ascontiguousarray(f(w_down)[0].reshape(NFF, 128, 1024)),
        "cosk": np.ascontiguousarray(cos.T), "sink": np.ascontiguousarray(sin.T),
    }
    wu = chunked(f(w_up)[0])
    wu = np.stack([np.concatenate([wu[:, :, j * 128:(j + 1) * 128], wu[:, :, 2816 + j * 128:2816 + (j + 1) * 128]], axis=2)
                   for j in range(NFF)], axis=0)
    shared["w_up"] = np.ascontiguousarray(wu)
    vec = np.zeros((128, NV), np.float32)
    col = lambda v: f(v).reshape(-1, 128).T
    vec[:, V_GMIX:V_GMIX + 8] = col(norm_mix_g[0])
    vec[:, V_GFFN:V_GFFN + 8] = col(norm_ffn_g[0])
    vec[:, V_GPLE:V_GPLE + 8] = col(ple_norm_g[0])
    vec[:, V_GFIN:V_GFIN + 8] = col(final_norm_g)
    vec[:, V_GQ:V_GQ + 2] = col(q_norm_g[0])
    vec[:, V_GKV:V_GKV + 1] = col(kv_norm_g[0])
    cw = f(conv_w)[0]
    for c in range(4):
        for k in range(3):
            vec[:, V_CONVW + 3 * c + k] = cw[k, c * 128:(c + 1) * 128]
    fw, fb = f(ffn_conv_w)[0], f(ffn_conv_b)[0]
    for j in range(2 * NFF):
        for k in range(3):
            vec[:, V_FFNW + 3 * j + k] = fw[k, j * 128:(j + 1) * 128]
        vec[:, V_FFNB + j] = fb[j * 128:(j + 1) * 128]
    maps = []
    for core in cores:
        b, c = core // 4, core % 4
        s0 = c * 2048
        xb = x[b]
        xp = np.zeros((S + 4, 1024), np.float32)
        xp[2:S + 2] = xb
        own = xp[s0:s0 + T]
        posq = np.clip(np.arange(s0 - 2, s0 - 2 + T), 0, S - 1)
        v = vec.copy()
        v[:, V_HM] = 1.0 if c > 0 else 0.0
        v[:, V_HM + 1] = 1.0 if c < 3 else 0.0
        m = dict(shared)
        m["xT_full"] = chunked(np.ascontiguousarray(xb.T))
        m["xT_own"] = chunked(np.ascontiguousarray(own.T))
        m["pT_own"] = chunked(np.ascontiguousarray(p[0, b, s0:s0 + 2048].T))
        m["vecs"] = v
        m["cosq"] = np.ascontiguousarray(cos[posq].T)
        m["sinq"] = np.ascontiguousarray(sin[posq].T)
        maps.append(m)
    return maps


_NC_CACHE = {}


def kernel(**inputs):
    if "nc" not in _NC_CACHE:
        _NC_CACHE["nc"] = build_program()
    nc = _NC_CACHE["nc"]
    maps = make_in_maps(**inputs)
    res = run_bass_kernel_spmd(nc, maps, core_ids=list(range(8)))
    out = np.empty((2, S, 1024), np.float32)
    for core in range(8):
        b, c = core // 4, core % 4
        o = res.results[core]["outT"]
        out[b, c * 2048:(c + 1) * 2048] = o.transpose(2, 1, 0).reshape(2048, 1024)
    return out
```

```python
import numpy as np
import ml_dtypes
import concourse.bass as bass
import concourse.mybir as mybir
from concourse.bass_utils import run_bass_kernel_spmd

F32 = mybir.dt.float32
BF16 = mybir.dt.bfloat16
AF = mybir.ActivationFunctionType
ALU = mybir.AluOpType

S = 8192
T = 2052
NH = 8
EPS = 1e-6
SCALE = 96.0 ** -0.5
NFF = 22
TILES = [(0, 512), (512, 512), (1024, 512), (1536, 512), (2048, 4)]
FT = [(1 + 510 * i, 512) for i in range(4)] + [(2041, 10)]

V_GMIX, V_GFFN, V_GPLE, V_GFIN, V_GQ, V_GKV, V_CONVW, V_FFNW, V_FFNB, V_HM, NV = (
    0, 8, 16, 24, 32, 34, 35, 47, 179, 223, 225)

ENGS = ("pe", "act", "dve", "pool", "sp")


class Prog:
    def __init__(self):
        self.ops = {e: [] for e in ENGS}
        self.cnt = {}
        self.seen = {e: {} for e in ENGS}
        self.res = {}
        self.dirty = set()

    def _need(self, eng, tok, waits):
        sk, val = tok
        if sk == "pe" and eng == "pe":
            return
        if self.seen[eng].get(sk, 0) >= val:
            return
        self.seen[eng][sk] = val
        waits.append(tok)

    def add(self, eng, fn, reads=(), writes=(), dma=None):
        waits = []
        for k in reads:
            r = self.res.get(k)
            if r and r[0]:
                self._need(eng, r[0], waits)
        for k in writes:
            r = self.res.get(k)
            if r:
                if r[0]:
                    self._need(eng, r[0], waits)
                for tk in r[1]:
                    self._need(eng, tk, waits)
        if dma is not None:
            sk, amt = ("d", dma), 16
        else:
            sk, amt = eng, 1
        val = self.cnt.get(sk, 0) + amt
        self.cnt[sk] = val
        self.dirty.add(sk)
        tok = (sk, val)
        for k in reads:
            self.res.setdefault(k, [None, []])[1].append(tok)
        for k in writes:
            self.res[k] = [tok, []]
        self.ops[eng].append((waits, fn, (sk, amt)))
        return tok

    def barrier(self):
        for e in ENGS:
            waits = []
            for sk in sorted(self.dirty, key=str):
                if sk == e:
                    continue
                self._need(e, (sk, self.cnt[sk]), waits)
            self.ops[e].append((waits, None, None))
        self.res = {}
        self.dirty = set()


def build_program(dbg=None):
    nc = bass.Bass("TRN2", target_bir_lowering=False)

    def din(name, shape):
        return nc.dram_tensor(name, shape, F32, kind="ExternalInput").ap()

    xT_full = din("xT_full", [128, 8, S])
    xT_own = din("xT_own", [128, 8, T])
    pT_own = din("pT_own", [128, 2, 2048])
    w_in = din("w_in", [128, 8, 1952])
    w_uq = din("w_uq", [128, 2, 768])
    w_ukv = din("w_ukv", [128, 1024])
    w_o = din("w_o", [128, 8, 1024])
    w_up = din("w_up", [NFF, 128, 8, 256])
    w_down = din("w_down", [NFF, 128, 1024])
    w_gate = din("w_gate", [128, 8, 1024])
    w_proj = din("w_proj", [128, 2, 1024])
    vecs_d = din("vecs", [128, NV])
    cosk = din("cosk", [32, S])
    sink = din("sink", [32, S])
    cosq = din("cosq", [32, T])
    sinq = din("sinq", [32, T])
    outT = nc.dram_tensor("outT", [128, 8, 2048], F32, kind="ExternalOutput").ap()
    dbg_out = None
    if dbg is not None:
        dbg_out = nc.dram_tensor("dbg", [128, dbg[1]], F32, kind="ExternalOutput").ap()

    P = Prog()
    AW = 52000
    dma_ids = {}

    def dsem(name):
        if name not in dma_ids:
            dma_ids[name] = len(dma_ids)
        return dma_ids[name]

    with nc.sbuf_tensor("arena", [128, AW], F32) as arena_t, \
            nc.psum_tensor("ps", [128, 4096], F32) as ps_t:
        arena = arena_t[:]
        ps = ps_t[:]

        def bank(i, n=512, p0=0, p1=128, nb=1):
            return ps[p0:p1, i * 512:i * 512 + (n if nb == 1 else nb * 512)]

        class Zone:
            def __init__(self, base, size):
                self.base, self.size, self.off = base, size, 0

            def reset(self):
                self.off = 0

            def f32(self, n):
                n2 = (n + 1) // 2 * 2
                assert self.off + n2 <= self.size, ("arena overflow", self.off, n2, self.size)
                ap = arena[:, self.base + self.off:self.base + self.off + n]
                self.off += n2
                return ap

            def bf16(self, n):
                w = (n + 3) // 4 * 2
                assert self.off + w <= self.size, ("arena overflow", self.off, w, self.size)
                ap = arena[:, self.base + self.off:self.base + self.off + w].bitcast(BF16)[:, 0:n]
                self.off += w
                return ap

        Z0 = Zone(0, 2400)
        Z1 = Zone(2400, 10300)
        Z2 = Zone(12700, 8208)
        Z3 = Zone(20908, AW - 20908)

        vec = Z0.f32(NV)
        ones = Z0.bf16(128)
        wuq = Z0.bf16(2 * 768).rearrange("p (c n) -> p c n", c=2)
        wuqr = Z0.bf16(2 * 8 * 96).rearrange("p (c h n) -> p c h n", c=2, h=8)
        wukv = Z0.bf16(1024)
        kvn = Z1.bf16(S)
        krope = Z1.bf16(S)
        qn = Z1.bf16(2 * T).rearrange("p (c n) -> p c n", c=2)
        yT = Z2.bf16(8 * T).rearrange("p (c n) -> p c n", c=8)

        def V(c0, n=1):
            return vec[:, c0:c0 + n]

        P.add("sp", lambda e: e.dma_start(out=vec, in_=vecs_d[:, :]), writes=["vec"], dma=dsem("vec"))
        P.add("pool", lambda e: e.dma_start(out=wuq, in_=w_uq[:, :, :]), writes=["wuq"], dma=dsem("wuq"))
        P.add("pool", lambda e: e.dma_start(out=wukv, in_=w_ukv[:, :]), writes=["wukv"], dma=dsem("wukv"))
        P.add("dve", lambda e: e.memset(ones, 1.0), writes=["ones"])
        P.add("dve", lambda e: e.memset(wuqr, 0.0), writes=["wuqr"])
        wuq4 = wuq.rearrange("p c (h n) -> p c h n", h=8)
        P.add("dve", lambda e: e.tensor_scalar(out=wuqr[:, :, :, 64:80], in0=wuq4[:, :, :, 80:96], scalar1=-1.0,
                                               scalar2=None, op0=ALU.mult), reads=["wuq"], writes=["wuqr"])
        P.add("dve", lambda e: e.tensor_copy(out=wuqr[:, :, :, 80:96], in_=wuq4[:, :, :, 64:80]),
              reads=["wuq"], writes=["wuqr"])
        P.add("pool", lambda e: e.memset(yT[:, :, 0:1], 0.0), writes=["yT0"])
        P.add("pool", lambda e: e.memset(yT[:, :, T - 1:T], 0.0), writes=["yT0"])

        Z3.reset()
        wA = Z3.bf16(8 * 320).rearrange("p (c n) -> p c n", c=8)
        A_xt = [Z3.f32(8 * 512).rearrange("p (c n) -> p c n", c=8) for _ in range(2)]
        A_sq = [Z3.bf16(8 * 512).rearrange("p (c n) -> p c n", c=8) for _ in range(2)]
        A_xg = [Z3.bf16(8 * 512).rearrange("p (c n) -> p c n", c=8) for _ in range(2)]
        A_t = [{k: Z3.f32(512) for k in ("sx", "rstd", "kvl", "sk", "rk", "ca", "cb", "cs", "sn")} for _ in range(2)]
        A_sqk = [Z3.bf16(512) for _ in range(2)]

        P.add("pool", lambda e: e.dma_start(out=wA[:, :, 0:128], in_=w_in[:, :, 1792:1920]), writes=["wA"], dma=dsem("wA"))
        P.add("pool", lambda e: e.dma_start(out=wA[:, :, 192:224], in_=w_in[:, :, 1920:1952]), writes=["wA1"], dma=dsem("wA"))
        P.add("dve", lambda e: e.memset(wA[:, :, 128:192], 0.0), writes=["wAz"])
        P.add("dve", lambda e: e.memset(wA[:, :, 224:288], 0.0), writes=["wAz"])
        P.add("dve", lambda e: e.tensor_scalar(out=wA[:, :, 288:304], in0=wA[:, :, 208:224], scalar1=-1.0, scalar2=None,
                                               op0=ALU.mult), reads=["wA1"], writes=["wAr"])
        P.add("dve", lambda e: e.tensor_copy(out=wA[:, :, 304:320], in_=wA[:, :, 192:208]), reads=["wA1"], writes=["wAr2"])
        gmix_b = V(V_GMIX, 8).unsqueeze(2).to_broadcast([128, 8, 512])

        def A_front(i):
            s = i % 2
            c0 = i * 512
            tt = A_t[s]
            P.add("sp", lambda e, s=s, c0=c0: e.dma_start(out=A_xt[s], in_=xT_full[:, :, c0:c0 + 512]),
                  writes=[("Axt", s)], dma=dsem(("Axt", s)))
            P.add("sp", lambda e, tt=tt, c0=c0: e.dma_start(out=tt["cs"][64:96, :], in_=cosk[:, c0:c0 + 512]),
                  writes=[("Acs", s)], dma=dsem(("Acs", s)))
            P.add("sp", lambda e, tt=tt, c0=c0: e.dma_start(out=tt["sn"][64:96, :], in_=sink[:, c0:c0 + 512]),
                  writes=[("Asn", s)], dma=dsem(("Asn", s)))
            P.add("act", lambda e, s=s: e.activation(out=A_sq[s], in_=A_xt[s], func=AF.Square),
                  reads=[("Axt", s)], writes=[("Asq", s)])
            P.add("dve", lambda e, s=s: e.tensor_tensor(out=A_xg[s], in0=A_xt[s], in1=gmix_b, op=ALU.mult),
                  reads=[("Axt", s), "vec"], writes=[("Axg", s)])

            def pe1(e, s=s):
                ins = None
                for c in range(8):
                    ins = e.matmul(bank(4 * s), lhsT=ones, rhs=A_sq[s][:, c, :], start=(c == 0), stop=(c == 7))
                return ins
            P.add("pe", pe1, reads=[("Asq", s), "ones"], writes=[("ps", 4 * s)])

            def pe2(e, s=s):
                ins = None
                for c in range(8):
                    ins = e.matmul(bank(4 * s + 1), lhsT=wA[:, c, 0:128], rhs=A_xg[s][:, c, :], start=(c == 0), stop=(c == 7))
                for c in range(8):
                    ins = e.matmul(bank(4 * s + 2, p1=96), lhsT=wA[:, c, 128:224], rhs=A_xg[s][:, c, :], start=(c == 0), stop=(c == 7))
                for c in range(8):
                    ins = e.matmul(bank(4 * s + 3, p1=96), lhsT=wA[:, c, 224:320], rhs=A_xg[s][:, c, :], start=(c == 0), stop=(c == 7))
                return ins
            P.add("pe", pe2, reads=[("Axg", s), "wA", "wA1", "wAz", "wAr", "wAr2"],
                  writes=[("ps", 4 * s + 1), ("ps", 4 * s + 2), ("ps", 4 * s + 3)])

        def A_back(i):
            s = i % 2
            c0 = i * 512
            tt = A_t[s]
            P.add("act", lambda e, s=s, tt=tt: e.activation(out=tt["sx"], in_=bank(4 * s), func=AF.Ln, scale=1.0 / 1024, bias=V_EPS),
                  reads=[("ps", 4 * s), "eps"], writes=[("Asx", s)])
            P.add("act", lambda e, tt=tt: e.activation(out=tt["rstd"], in_=tt["sx"], func=AF.Exp, scale=-0.5), reads=[("Asx", s)], writes=[("Arstd", s)])
            P.add("dve", lambda e, s=s, tt=tt: e.tensor_tensor(out=tt["kvl"], in0=bank(4 * s + 1), in1=tt["rstd"], op=ALU.mult),
                  reads=[("ps", 4 * s + 1), ("Arstd", s)], writes=[("Akvl", s)])
            P.add("act", lambda e, s=s, tt=tt: e.activation(out=A_sqk[s], in_=tt["kvl"], func=AF.Square),
                  reads=[("Akvl", s)], writes=[("Asqk", s)])
            P.add("pe", lambda e, s=s: e.matmul(bank(4 * s), lhsT=ones, rhs=A_sqk[s], start=True, stop=True),
                  reads=[("Asqk", s), "ones"], writes=[("ps", 4 * s)])
            P.add("act", lambda e, s=s, tt=tt: e.activation(out=tt["sk"], in_=bank(4 * s), func=AF.Ln, scale=1.0 / 128, bias=V_EPS),
                  reads=[("ps", 4 * s), "eps"], writes=[("Ask", s)])
            P.add("act", lambda e, tt=tt: e.activation(out=tt["rk"], in_=tt["sk"], func=AF.Exp, scale=-0.5), reads=[("Ask", s)], writes=[("Ark", s)])
            P.add("dve", lambda e, tt=tt, c0=c0: e.scalar_tensor_tensor(out=kvn[:, c0:c0 + 512], in0=tt["kvl"], scalar=V(V_GKV),
                                                                         in1=tt["rk"], op0=ALU.mult, op1=ALU.mult),
                  reads=[("Akvl", s), ("Ark", s), "vec"], writes=[("kvn", i)])
            P.add("dve", lambda e, s=s, tt=tt: e.tensor_tensor(out=tt["ca"][64:96, :], in0=bank(4 * s + 2, p0=64, p1=96),
                                                               in1=tt["cs"][64:96, :], op=ALU.mult),
                  reads=[("ps", 4 * s + 2), ("Acs", s)], writes=[("Aca", s)])
            P.add("dve", lambda e, s=s, tt=tt: e.tensor_tensor(out=tt["cb"][64:96, :], in0=bank(4 * s + 3, p0=64, p1=96),
                                                               in1=tt["sn"][64:96, :], op=ALU.mult),
                  reads=[("ps", 4 * s + 3), ("Asn", s)], writes=[("Acb", s)])
            P.add("dve", lambda e, tt=tt: e.tensor_tensor(out=tt["ca"][64:96, :], in0=tt["ca"][64:96, :], in1=tt["cb"][64:96, :], op=ALU.add),
                  reads=[("Aca", s), ("Acb", s)], writes=[("Aca", s)])
            P.add("dve", lambda e, tt=tt, c0=c0: e.tensor_tensor(out=krope[64:96, c0:c0 + 512], in0=tt["ca"][64:96, :], in1=tt["rstd"][64:96, :], op=ALU.mult),
                  reads=[("Aca", s), ("Arstd", s)], writes=[("krope", i)])

        for i in range(17):
            if i < 16:
                A_front(i)
            if i >= 1:
                A_back(i - 1)

        P.barrier()
        if dbg and dbg[0] == "A":
            P.add("sp", lambda e: e.dma_start(out=dbg_out[:, 0:S].bitcast(BF16)[:, 0:S], in_=kvn), dma=dsem("dbg"))
            P.add("sp", lambda e: e.dma_start(out=dbg_out[:, S:2 * S].bitcast(BF16)[:, 0:S], in_=krope), dma=dsem("dbg"))
            P.barrier()
            return finish(nc, P, dma_ids)

        Z3.reset()
        xgT = Z3.bf16(8 * T).rearrange("p (c n) -> p c n", c=8)
        rstd1 = Z3.f32(T)
        mark = Z3.off
        B_xt = [Z3.f32(8 * 512).rearrange("p (c n) -> p c n", c=8) for _ in range(2)]
        B_sq = [Z3.bf16(8 * 512).rearrange("p (c n) -> p c n", c=8) for _ in range(2)]
        B_sx = [Z3.f32(512) for _ in range(2)]
        for ti, (c0, n) in enumerate(TILES):
            s = ti % 2
            P.add("sp", lambda e, s=s, c0=c0, n=n: e.dma_start(out=B_xt[s][:, :, 0:n], in_=xT_own[:, :, c0:c0 + n]),
                  writes=[("Bxt", s)], dma=dsem(("Bxt", s)))
            P.add("act", lambda e, s=s, n=n: e.activation(out=B_sq[s][:, :, 0:n], in_=B_xt[s][:, :, 0:n], func=AF.Square),
                  reads=[("Bxt", s)], writes=[("Bsq", s)])
            P.add("dve", lambda e, s=s, c0=c0, n=n: e.tensor_tensor(out=xgT[:, :, c0:c0 + n], in0=B_xt[s][:, :, 0:n],
                                                                    in1=V(V_GMIX, 8).unsqueeze(2).to_broadcast([128, 8, n]), op=ALU.mult),
                  reads=[("Bxt", s), "vec"], writes=[("xgT", ti)])

            def pe1(e, s=s, n=n):
                ins = None
                for c in range(8):
                    ins = e.matmul(bank(s, n), lhsT=ones, rhs=B_sq[s][:, c, 0:n], start=(c == 0), stop=(c == 7))
                return ins
            P.add("pe", pe1, reads=[("Bsq", s), "ones"], writes=[("ps", s)])
            P.add("act", lambda e, s=s, n=n: e.activation(out=B_sx[s][:, 0:n], in_=bank(s, n), func=AF.Ln, scale=1.0 / 1024, bias=V_EPS),
                  reads=[("ps", s), "eps"], writes=[("Bsx", s)])
            P.add("act", lambda e, s=s, c0=c0, n=n: e.activation(out=rstd1[:, c0:c0 + n], in_=B_sx[s][:, 0:n], func=AF.Exp, scale=-0.5),
                  reads=[("Bsx", s)], writes=[("rstd1", ti)])
        P.barrier()
        Z3.off = mark
        wS = [Z3.bf16(8 * 384).rearrange("p (c n) -> p c n", c=8) for _ in range(2)]
        wQ = Z3.bf16(8 * 256).rearrange("p (c n) -> p c n", c=8)
        B_xs = [Z3.f32(512) for _ in range(2)]
        B_t1 = [Z3.f32(512) for _ in range(2)]
        B_u = Z3.f32(T)
        rstd2 = Z3.f32(T)
        P.add("dve", lambda e: e.tensor_tensor(out=rstd2, in0=rstd1, in1=rstd1, op=ALU.mult), writes=["rstd2"])
        B_bg = Z3.f32(T)
        B_c = Z3.f32(T)
        ql = Z3.f32(2 * T).rearrange("p (c n) -> p c n", c=2)
        B_sqq = [Z3.bf16(512) for _ in range(2)]
        B_sxq = [Z3.f32(512)] * 2
        B_rq = [Z3.f32(512)] * 2
        P.add("pool", lambda e: e.dma_start(out=wQ, in_=w_in[:, :, 1536:1792]), writes=["wQ"], dma=dsem("wQ"))
        pb = [0]

        def nextbank():
            pb[0] = (pb[0] + 1) % 8
            return pb[0]

        for k in range(4):
            ws = k % 2
            for j, src in enumerate((k * 128, 1024 + k * 128, 512 + k * 128)):
                P.add("pool", lambda e, ws=ws, j=j, src=src: e.dma_start(out=wS[ws][:, :, j * 128:(j + 1) * 128], in_=w_in[:, :, src:src + 128]),
                      writes=[("wS", ws, j)], dma=dsem(("wS", ws)))
            for ti, (c0, n) in enumerate(TILES):
                s = ti % 2
                bx, bc, bb = nextbank(), nextbank(), nextbank()

                def pe3(e, ws=ws, c0=c0, n=n, bx=bx, bc=bc, bb=bb):
                    ins = None
                    for j, b in enumerate((bx, bc, bb)):
                        for c in range(8):
                            ins = e.matmul(bank(b, n), lhsT=wS[ws][:, c, j * 128:(j + 1) * 128], rhs=xgT[:, c, c0:c0 + n],
                                           start=(c == 0), stop=(c == 7))
                    return ins
                P.add("pe", pe3, reads=[("wS", ws, 0), ("wS", ws, 1), ("wS", ws, 2), ("xgT", ti)], writes=[("ps", bx), ("ps", bc), ("ps", bb)])
                P.add("act", lambda e, s=s, n=n, bx=bx: e.activation(out=B_xs[s][:, 0:n], in_=bank(bx, n), func=AF.Copy),
                      reads=[("ps", bx)], writes=[("Bxs", s)])
                P.add("dve", lambda e, s=s, c0=c0, n=n, bc=bc: e.tensor_tensor(out=B_t1[s][:, 0:n], in0=bank(bc, n), in1=B_xs[s][:, 0:n], op=ALU.mult),
                      reads=[("ps", bc), ("Bxs", s)], writes=[("Bt1", s)])
                P.add("dve", lambda e, s=s, c0=c0, n=n: e.tensor_tensor(out=B_u[:, c0:c0 + n], in0=B_t1[s][:, 0:n], in1=rstd2[:, c0:c0 + n], op=ALU.mult),
                      reads=[("Bt1", s), "rstd2"], writes=[("Bu", ti)])
                P.add("dve", lambda e, c0=c0, n=n, bb=bb: e.tensor_tensor(out=B_bg[:, c0:c0 + n], in0=bank(bb, n), in1=rstd1[:, c0:c0 + n], op=ALU.mult),
                      reads=[("ps", bb), ("rstd1", ti)], writes=[("Bbg", ti)])
            allu = [("Bu", ti) for ti in range(5)]
            allbg = [("Bbg", ti) for ti in range(5)]
            cw = V_CONVW + 3 * k
            P.add("dve", lambda e, cw=cw: e.tensor_scalar(out=B_c[:, 1:T - 1], in0=B_u[:, 1:T - 1], scalar1=V(cw + 1), scalar2=None, op0=ALU.mult),
                  reads=allu + ["vec"], writes=["Bc"])
            P.add("dve", lambda e, cw=cw: e.scalar_tensor_tensor(out=B_c[:, 1:T - 1], in0=B_u[:, 0:T - 2], scalar=V(cw), in1=B_c[:, 1:T - 1],
                                                                  op0=ALU.mult, op1=ALU.add), reads=allu + ["Bc"], writes=["Bc"])
            P.add("dve", lambda e, cw=cw: e.scalar_tensor_tensor(out=B_c[:, 1:T - 1], in0=B_u[:, 2:T], scalar=V(cw + 2), in1=B_c[:, 1:T - 1],
                                                                  op0=ALU.mult, op1=ALU.add), reads=allu + ["Bc"], writes=["Bc"])
            P.add("dve", lambda e, k=k: e.tensor_tensor(out=yT[:, k, 1:T - 1], in0=B_bg[:, 1:T - 1], in1=B_c[:, 1:T - 1], op=ALU.mult),
                  reads=allbg + ["Bc"], writes=[("yT", k)])
        for ti, (c0, n) in enumerate(TILES):
            s = ti % 2
            bq = nextbank()
            for kc in range(2):
                b = nextbank()

                def pe4(e, kc=kc, c0=c0, n=n, b=b):
                    ins = None
                    for c in range(8):
                        ins = e.matmul(bank(b, n), lhsT=wQ[:, c, kc * 128:(kc + 1) * 128], rhs=xgT[:, c, c0:c0 + n], start=(c == 0), stop=(c == 7))
                    return ins
                P.add("pe", pe4, reads=["wQ", ("xgT", ti)], writes=[("ps", b)])
                P.add("dve", lambda e, kc=kc, c0=c0, n=n, b=b: e.tensor_tensor(out=ql[:, kc, c0:c0 + n], in0=bank(b, n), in1=rstd1[:, c0:c0 + n], op=ALU.mult),
                      reads=[("ps", b), ("rstd1", ti)], writes=[("ql", ti, kc)])
                P.add("act", lambda e, s=s, kc=kc, c0=c0, n=n: e.activation(out=B_sqq[s][:, 0:n], in_=ql[:, kc, c0:c0 + n], func=AF.Square),
                      reads=[("ql", ti, kc)], writes=[("Bsqq", s)])
                P.add("pe", lambda e, s=s, kc=kc, n=n, bq=bq: e.matmul(bank(bq, n), lhsT=ones, rhs=B_sqq[s][:, 0:n], start=(kc == 0), stop=(kc == 1)),
                      reads=[("Bsqq", s), "ones"], writes=[("ps", bq)])
            P.add("act", lambda e, s=s, n=n, bq=bq: e.activation(out=B_sxq[s][:, 0:n], in_=bank(bq, n), func=AF.Ln, scale=1.0 / 256, bias=V_EPS),
                  reads=[("ps", bq), "eps"], writes=[("Bsxq", 0)])
            P.add("act", lambda e, s=s, n=n: e.activation(out=B_rq[s][:, 0:n], in_=B_sxq[s][:, 0:n], func=AF.Exp, scale=-0.5), reads=[("Bsxq", 0)], writes=[("Brq", 0)])
            for kc in range(2):
                P.add("dve", lambda e, s=s, kc=kc, c0=c0, n=n: e.scalar_tensor_tensor(out=qn[:, kc, c0:c0 + n], in0=ql[:, kc, c0:c0 + n], scalar=V(V_GQ + kc),
                                                                                      in1=B_rq[s][:, 0:n], op0=ALU.mult, op1=ALU.mult),
                      reads=[("ql", ti, kc), ("Brq", 0), "vec"], writes=[("qn", ti)])
        P.barrier()
        if dbg and dbg[0] == "B":
            P.add("sp", lambda e: e.dma_start(out=dbg_out[:, 0:4 * T].bitcast(BF16)[:, 0:8 * T].rearrange("p (c n) -> p c n", c=8), in_=yT), dma=dsem("dbg"))
            P.add("sp", lambda e: e.dma_start(out=dbg_out[:, 4 * T:5 * T].bitcast(BF16)[:, 0:2 * T].rearrange("p (c n) -> p c n", c=2), in_=qn), dma=dsem("dbg"))
            P.barrier()
            return finish(nc, P, dma_ids)

        Z3.reset()
        Kh = [Z3.bf16(S) for _ in range(2)]
        Vh = [Z3.bf16(64 * 128).rearrange("p (k d) -> p k d", d=128) for _ in range(2)]
        Qh = [Z3.bf16(T) for _ in range(2)]
        Pb = [Z3.bf16(1024) for _ in range(4)]
        cq = Z3.f32(T)
        sq_ = Z3.f32(T)
        R_t1 = [Z3.f32(512) for _ in range(2)]
        R_t2 = [Z3.f32(512) for _ in range(2)]
        rec = Z3.f32(1024)
        ocp = [Z3.f32(1024) for _ in range(2)]
        ocn = [0]
        Ph = Z3.bf16(128)
        rech = Z3.f32(2)
        P.add("sp", lambda e: e.dma_start(out=cq[64:96, :], in_=cosq[:, :]), writes=["cq"], dma=dsem("cq"))
        P.add("sp", lambda e: e.dma_start(out=sq_[64:96, :], in_=sinq[:, :]), writes=["sq"], dma=dsem("sq"))
        for s in range(2):
            P.add("pool", lambda e, s=s: e.memset(Vh[s][:, :, 64:128], 1.0), writes=[("Vones", s)])
        bb = [0]

        def bbank():
            bb[0] = 1 - bb[0]
            return 6 + bb[0]

        def build_steps(h):
            s = h % 2
            steps = []
            steps.append(lambda: P.add("sp", lambda e: e.dma_start(out=Kh[s][64:96, :], in_=krope[64:96, :]),
                                       reads=[("krope", i) for i in range(16)] if h < 2 else [], writes=[("Khr", s)], dma=dsem(("Khr", s))))
            for t in range(16):
                def st(t=t):
                    b = bbank()
                    P.add("pe", lambda e: e.matmul(bank(b, p1=64), lhsT=wukv[:, h * 128:h * 128 + 64], rhs=kvn[:, t * 512:(t + 1) * 512], start=True, stop=True),
                          reads=["wukv"] + ([("kvn", t)] if h < 2 else []), writes=[("ps", b)])
                    P.add("dve", lambda e: e.tensor_copy(out=Kh[s][0:64, t * 512:(t + 1) * 512], in_=bank(b, p1=64)),
                          reads=[("ps", b)], writes=[("Kh", s, t)])
                steps.append(st)
            for g in range(8):
                def st(g=g):
                    b = bbank()

                    def pev(e):
                        ins = None
                        for j in range(8):
                            kt = 8 * g + j
                            ins = e.matmul(bank(b)[:, j * 64:(j + 1) * 64], lhsT=kvn[:, kt * 128:(kt + 1) * 128], rhs=wukv[:, h * 128 + 64:h * 128 + 128],
                                           start=True, stop=True)
                        return ins
                    P.add("pe", pev, reads=["wukv"] + ([("kvn", t) for t in range(16)] if h < 2 else []), writes=[("ps", b)])
                    P.add("dve", lambda e: e.tensor_copy(out=Vh[s][:, 8 * g:8 * g + 8, 0:64], in_=bank(b).rearrange("p (j d) -> p j d", d=64)),
                          reads=[("ps", b)], writes=[("Vh", s, g)])
                steps.append(st)
            for ti, (c0, n) in enumerate(TILES):
                def st(ti=ti, c0=c0, n=n):
                    ba, bb_ = bbank(), bbank()
                    r = ti % 2

                    def peq(e):
                        ins = None
                        for kc in range(2):
                            ins = e.matmul(bank(ba, n, p1=96), lhsT=wuq[:, kc, h * 96:(h + 1) * 96], rhs=qn[:, kc, c0:c0 + n], start=(kc == 0), stop=(kc == 1))
                        for kc in range(2):
                            ins = e.matmul(bank(bb_, n, p1=96), lhsT=wuqr[:, kc, h, :], rhs=qn[:, kc, c0:c0 + n], start=(kc == 0), stop=(kc == 1))
                        return ins
                    P.add("pe", peq, reads=["wuq", "wuqr"] + ([("qn", ti)] if h < 2 else []), writes=[("ps", ba), ("ps", bb_)])
                    P.add("dve", lambda e: e.tensor_copy(out=Qh[s][0:64, c0:c0 + n], in_=bank(ba, n, p1=64)), reads=[("ps", ba)], writes=[("Qh", s, ti)])
                    P.add("dve", lambda e: e.tensor_tensor(out=R_t1[r][64:96, 0:n], in0=bank(ba, n, p0=64, p1=96), in1=cq[64:96, c0:c0 + n], op=ALU.mult),
                          reads=[("ps", ba), "cq"], writes=[("Rt1", r)])
                    P.add("dve", lambda e: e.tensor_tensor(out=R_t2[r][64:96, 0:n], in0=bank(bb_, n, p0=64, p1=96), in1=sq_[64:96, c0:c0 + n], op=ALU.mult),
                          reads=[("ps", bb_), "sq"], writes=[("Rt2", r)])
                    P.add("dve", lambda e: e.tensor_tensor(out=Qh[s][64:96, c0:c0 + n], in0=R_t1[r][64:96, 0:n], in1=R_t2[r][64:96, 0:n], op=ALU.add),
                          reads=[("Rt1", r), ("Rt2", r)], writes=[("Qh", s, ti)])
                steps.append(st)
            return steps

        def khv_reads(s):
            return ([("Khr", s)] + [("Kh", s, t) for t in range(16)] + [("Qh", s, ti) for ti in range(5)])

        def v_reads(s):
            return [("Vh", s, g) for g in range(8)] + [("Vones", s)]

        for st in build_steps(0):
            st()
        for h in range(NH):
            s = h % 2
            pending = build_steps(h + 1) if h + 1 < NH else []
            kreads, vreads = khv_reads(s), v_reads(s)
            dst_p0 = (h % 2) * 64
            ych = 4 + h // 2
            bS = bbank()

            def pehs(e, s=s, bS=bS):
                ins = None
                for kt in range(64):
                    ins = e.matmul(bank(bS)[:, 2 * kt:2 * kt + 2], lhsT=Kh[s][0:96, kt * 128:(kt + 1) * 128], rhs=Qh[s][0:96, 1:T - 1:T - 3], start=True, stop=True)
                return ins
            P.add("pe", pehs, reads=kreads, writes=[("ps", bS)])
            P.add("act", lambda e, bS=bS: e.activation(out=Ph, in_=bank(bS, 128), func=AF.Exp, scale=SCALE), reads=[("ps", bS)], writes=["Ph"])
            bO = bbank()

            def peho(e, s=s, bO=bO):
                ins = None
                for kt in range(64):
                    ins = e.matmul(bank(bO, 2), lhsT=Vh[s][:, kt, :], rhs=Ph[:, 2 * kt:2 * kt + 2], start=(kt == 0), stop=(kt == 63))
                return ins
            P.add("pe", peho, reads=vreads + ["Ph"], writes=[("ps", bO)])
            P.add("dve", lambda e, bO=bO: e.reciprocal(out=rech[0:64, :], in_=bank(bO, 2, p0=64, p1=128)), reads=[("ps", bO)], writes=["rech"])
            P.add("dve", lambda e, bO=bO, dst_p0=dst_p0, ych=ych: e.tensor_tensor(out=yT[dst_p0:dst_p0 + 64, ych, 1:T - 1:T - 3], in0=bank(bO, 2, p1=64),
                                                                                  in1=rech[0:64, :], op=ALU.mult),
                  reads=[("ps", bO), "rech"], writes=[("yTh", h)])
            seq = [(qh, kt) for qh in range(2) for kt in range(64)]
            nsteps = len(seq)

            def issue_S(idx, s=s):
                qh, kt = seq[idx]
                sb = idx % 2
                q0 = 2 + 1024 * qh

                def pes(e):
                    ins = None
                    for j in range(2):
                        ins = e.matmul(bank(2 * sb + j), lhsT=Kh[s][0:96, kt * 128:(kt + 1) * 128], rhs=Qh[s][0:96, q0 + 512 * j:q0 + 512 * (j + 1)],
                                       start=True, stop=True)
                    return ins
                P.add("pe", pes, reads=kreads, writes=[("ps", 2 * sb), ("ps", 2 * sb + 1)])

            def issue_E(idx, s=s):
                sb = idx % 2
                pi = idx % 4
                P.add("act", lambda e: e.activation(out=Pb[pi], in_=bank(2 * sb, nb=2), func=AF.Exp, scale=SCALE),
                      reads=[("ps", 2 * sb), ("ps", 2 * sb + 1)], writes=[("Pb", pi)])

            def issue_PV(idx, s=s, dst_p0=dst_p0, ych=ych, h=h):
                qh, kt = seq[idx]
                q0 = 2 + 1024 * qh
                pi = idx % 4

                def pepv(e):
                    ins = None
                    for j in range(2):
                        ins = e.matmul(bank(4 + j), lhsT=Vh[s][:, kt, :], rhs=Pb[pi][:, 512 * j:512 * (j + 1)], start=(kt == 0), stop=(kt == 63))
                    return ins
                P.add("pe", pepv, reads=vreads + [("Pb", pi)], writes=[("ps", 4), ("ps", 5)])
                if kt == 63:
                    oc = ocp[ocn[0] % 2]
                    ocn[0] += 1
                    P.add("dve", lambda e: e.tensor_copy(out=oc, in_=bank(4, nb=2)), reads=[("ps", 4), ("ps", 5)], writes=[("oc", id(oc))])
                    P.add("dve", lambda e: e.reciprocal(out=rec[0:64, :], in_=oc[64:128, :]), reads=[("oc", id(oc))], writes=["rec"])
                    P.add("dve", lambda e: e.tensor_tensor(out=yT[dst_p0:dst_p0 + 64, ych, q0:q0 + 1024], in0=oc[0:64, :], in1=rec[0:64, :], op=ALU.mult),
                          reads=[("oc", id(oc)), "rec"], writes=[("yTm", h, qh)])

            issue_S(0)
            for idx in range(nsteps):
                if idx + 1 < nsteps:
                    issue_S(idx + 1)
                issue_E(idx)
                if idx >= 1:
                    issue_PV(idx - 1)
                if pending and idx % 3 == 2:
                    pending.pop(0)()
            issue_PV(nsteps - 1)
            while pending:
                pending.pop(0)()
        P.barrier()
        if dbg and dbg[0] == "C":
            P.add("sp", lambda e: e.dma_start(out=dbg_out[:, 0:4 * T].bitcast(BF16)[:, 0:8 * T].rearrange("p (c n) -> p c n", c=8), in_=yT), dma=dsem("dbg"))
            P.barrier()
            return finish(nc, P, dma_ids)


        Z3.reset()
        x1 = Z3.f32(8 * T).rearrange("p (c n) -> p c n", c=8)
        hf = Z3.bf16(8 * T).rearrange("p (c n) -> p c n", c=8)
        wo_raw = Z3.bf16(8 * 1024)
        wo = wo_raw.rearrange("p (c n) -> p c n", c=8)
        stg = [Z3.f32(4 * 512).rearrange("p (c n) -> p c n", c=4) for _ in range(1)]
        Z1.reset()
        xr = [Z1.f32(T) for _ in range(2)]
        rf = Z1.f32(T)
        sqf = [Z1.bf16(8 * 512).rearrange("p (c n) -> p c n", c=8) for _ in range(2)]
        D_sx = [wo_raw.bitcast(F32)[:, 0:512], wo_raw.bitcast(F32)[:, 512:1024]]
        P.add("pool", lambda e: e.dma_start(out=wo, in_=w_o[:, :, :]), writes=["wo"], dma=dsem("wo"))
        for d in range(8):
            r = d % 2
            P.add("sp", lambda e, d=d, r=r: e.dma_start(out=xr[r], in_=xT_own[:, d, :]), writes=[("xr", r)], dma=dsem(("xr", r)))
            for ti, (c0, n) in enumerate(TILES):
                b = nextbank()

                def pewo(e, d=d, c0=c0, n=n, b=b):
                    ins = None
                    for c in range(8):
                        ins = e.matmul(bank(b, n), lhsT=wo[:, c, d * 128:(d + 1) * 128], rhs=yT[:, c, c0:c0 + n], start=(c == 0), stop=(c == 7))
                    return ins
                P.add("pe", pewo, reads=["wo"], writes=[("ps", b)])
                P.add("dve", lambda e, d=d, r=r, c0=c0, n=n, b=b: e.tensor_tensor(out=x1[:, d, c0:c0 + n], in0=bank(b, n), in1=xr[r][:, c0:c0 + n], op=ALU.add),
                      reads=[("ps", b), ("xr", r)], writes=[("x1", d, ti)])

        def rmsnorm_to(src, dst, gcol, tiles, rbuf, tag, sqf, D_sx):
            for ti, (c0, n) in enumerate(tiles):
                s = ti % 2
                bq = nextbank()
                P.add("act", lambda e, s=s, c0=c0, n=n: e.activation(out=sqf[s][:, :, 0:n], in_=src[:, :, c0:c0 + n], func=AF.Square),
                      reads=[(tag, d, "all") for d in range(8)] + [("x1", d, ti) for d in range(8)], writes=[("sqf", s)])

                def pen(e, s=s, n=n, bq=bq):
                    ins = None
                    for c in range(8):
                        ins = e.matmul(bank(bq, n), lhsT=ones, rhs=sqf[s][:, c, 0:n], start=(c == 0), stop=(c == 7))
                    return ins
                P.add("pe", pen, reads=[("sqf", s), "ones"], writes=[("ps", bq)])
                P.add("act", lambda e, s=s, n=n, bq=bq: e.activation(out=D_sx[s][:, 0:n], in_=bank(bq, n), func=AF.Ln, scale=1.0 / 1024, bias=V_EPS),
                      reads=[("ps", bq)], writes=[("Dsx", s), "wo"])
                P.add("act", lambda e, s=s, c0=c0, n=n: e.activation(out=rbuf[:, c0:c0 + n], in_=D_sx[s][:, 0:n], func=AF.Exp, scale=-0.5), reads=[("Dsx", s)], writes=[("rbuf", ti)])
                if dst is not None:
                    for d in range(8):
                        P.add("dve", lambda e, d=d, c0=c0, n=n: e.scalar_tensor_tensor(out=dst[:, d, c0:c0 + n], in0=src[:, d, c0:c0 + n], scalar=V(gcol + d),
                                                                                       in1=rbuf[:, c0:c0 + n], op0=ALU.mult, op1=ALU.mult),
                              reads=[("rbuf", ti), (tag, d, "all"), ("x1", d, ti), "vec"], writes=[("hf", d, ti)])

        rmsnorm_to(x1, hf, V_GFFN, TILES, rf, "x1", sqf, D_sx)
        P.add("dve", lambda e: e.tensor_scalar(out=hf[:, :, 1:2], in0=hf[:, :, 1:2], scalar1=V(V_HM), scalar2=None, op0=ALU.mult),
              reads=[("hf", d, 0) for d in range(8)] + ["vec"], writes=[("hf", d, 0) for d in range(8)])
        P.add("dve", lambda e: e.tensor_scalar(out=hf[:, :, T - 2:T - 1], in0=hf[:, :, T - 2:T - 1], scalar1=V(V_HM + 1), scalar2=None, op0=ALU.mult),
              reads=[("hf", d, 4) for d in range(8)] + ["vec"], writes=[("hf", d, 4) for d in range(8)])
        P.barrier()
        if dbg and dbg[0] == "D":
            P.add("sp", lambda e: e.dma_start(out=dbg_out[:, 0:8 * T].rearrange("p (c n) -> p c n", c=8), in_=x1), dma=dsem("dbg"))
            P.barrier()
            return finish(nc, P, dma_ids)

        ZF = Zone(2400, 18508)
        G = 4
        wup = [ZF.bf16(8 * 256).rearrange("p (c n) -> p c n", c=8) for _ in range(3)]
        wdn = [ZF.bf16(1024) for _ in range(8)]
        a0g = [ZF.f32(512) for _ in range(2)]
        a0u = [ZF.f32(512) for _ in range(2)]
        sgb = [ZF.f32(512) for _ in range(2)]
        actT = [[ZF.bf16(2048) for _ in range(G)] for _ in range(2)]
        hf_all = [("hf", d, ti) for d in range(8) for ti in range(5)]

        def load_w(j):
            P.add("pool", lambda e, j=j: e.dma_start(out=wup[j % 3], in_=w_up[j, :, :, :]), writes=[("wup", j % 3)], dma=dsem(("wup", j % 3)))
            P.add("pool", lambda e, j=j: e.dma_start(out=wdn[j % 8], in_=w_down[j, :, :]), writes=[("wdn", j % 8)], dma=dsem(("wdn", j % 8)))

        cnt2 = [0]

        def U(j):
            if j + 2 < NFF:
                load_w(j + 2)
            gs, jj = (j // G) % 2, j % G
            for fi, (c0, n) in enumerate(FT):
                s = cnt2[0] % 2
                cnt2[0] += 1
                bg_, bu_ = nextbank(), nextbank()

                def peu(e, c0=c0, n=n, bg_=bg_, bu_=bu_):
                    ins = None
                    for c in range(8):
                        ins = e.matmul(bank(bg_, n), lhsT=wup[j % 3][:, c, 0:128], rhs=hf[:, c, c0:c0 + n], start=(c == 0), stop=(c == 7))
                    for c in range(8):
                        ins = e.matmul(bank(bu_, n), lhsT=wup[j % 3][:, c, 128:256], rhs=hf[:, c, c0:c0 + n], start=(c == 0), stop=(c == 7))
                    return ins
                P.add("pe", peu, reads=[("wup", j % 3)] + hf_all, writes=[("ps", bg_), ("ps", bu_)])
                wg_, wu_ = V_FFNW + 3 * j, V_FFNW + 3 * (j + NFF)
                P.add("act", lambda e, s=s, n=n, bg_=bg_, wg_=wg_: e.activation(out=a0g[s][:, 0:n], in_=bank(bg_, n), func=AF.Identity, scale=V(wg_ + 1), bias=V(V_FFNB + j)),
                      reads=[("ps", bg_), "vec"], writes=[("a0g", s)])
                P.add("act", lambda e, s=s, n=n, bu_=bu_, wu_=wu_: e.activation(out=a0u[s][:, 0:n], in_=bank(bu_, n), func=AF.Identity, scale=V(wu_ + 1), bias=V(V_FFNB + j + NFF)),
                      reads=[("ps", bu_), "vec"], writes=[("a0u", s)])
                for (buf, key, bk, wv) in ((a0g, "a0g", bg_, wg_), (a0u, "a0u", bu_, wu_)):
                    P.add("dve", lambda e, s=s, n=n, buf=buf, bk=bk, wv=wv: e.scalar_tensor_tensor(out=buf[s][:, 1:n - 1], in0=bank(bk, n)[:, 0:n - 2], scalar=V(wv),
                                                                                                   in1=buf[s][:, 1:n - 1], op0=ALU.mult, op1=ALU.add),
                          reads=[("ps", bk), (key, s), "vec"], writes=[(key, s)])
                    P.add("dve", lambda e, s=s, n=n, buf=buf, bk=bk, wv=wv: e.scalar_tensor_tensor(out=buf[s][:, 1:n - 1], in0=bank(bk, n)[:, 2:n], scalar=V(wv + 2),
                                                                                                   in1=buf[s][:, 1:n - 1], op0=ALU.mult, op1=ALU.add),
                          reads=[("ps", bk), (key, s), "vec"], writes=[(key, s)])
                P.add("act", lambda e, s=s, n=n: e.activation(out=sgb[s][:, 0:n - 2], in_=a0g[s][:, 1:n - 1], func=AF.Silu),
                      reads=[("a0g", s)], writes=[("sgb", s)])
                o0 = c0 - 1
                P.add("dve", lambda e, s=s, n=n, o0=o0, gs=gs, jj=jj: e.tensor_tensor(out=actT[gs][jj][:, o0:o0 + n - 2], in0=sgb[s][:, 0:n - 2], in1=a0u[s][:, 1:n - 1], op=ALU.mult),
                      reads=[("sgb", s), ("a0u", s)], writes=[("actT", gs, jj, fi)])

        def Dn(g):
            gs = g % 2
            js = list(range(g * G, min((g + 1) * G, NFF)))
            areads = [("actT", gs, j % G, fi) for j in js for fi in range(5)] + [("wdn", j % 8) for j in js]
            for d in range(8):
                for q in range(4):
                    b = nextbank()

                    def ped(e, d=d, q=q, b=b):
                        ins = None
                        for ii, j in enumerate(js):
                            ins = e.matmul(bank(b), lhsT=wdn[j % 8][:, d * 128:(d + 1) * 128], rhs=actT[gs][j % G][:, q * 512:(q + 1) * 512],
                                           start=(ii == 0), stop=(ii == len(js) - 1))
                        return ins
                    P.add("pe", ped, reads=areads, writes=[("ps", b)])
                    P.add("dve", lambda e, d=d, q=q, b=b: e.tensor_tensor(out=x1[:, d, 2 + q * 512:2 + (q + 1) * 512], in0=bank(b), in1=x1[:, d, 2 + q * 512:2 + (q + 1) * 512], op=ALU.add),
                          reads=[("ps", b), ("x2", d, q)], writes=[("x2", d, q)])

        load_w(0)
        load_w(1)
        ngroups = (NFF + G - 1) // G
        for j in range(NFF):
            U(j)
            if j % G == 0 and j > 0:
                Dn(j // G - 1)
        Dn(ngroups - 1)
        P.barrier()
        if dbg and dbg[0] == "E":
            P.add("sp", lambda e: e.dma_start(out=dbg_out[:, 0:8 * T].rearrange("p (c n) -> p c n", c=8), in_=x1), dma=dsem("dbg"))
            P.barrier()
            return finish(nc, P, dma_ids)

        ZF.reset()
        wg = ZF.bf16(8 * 1024).rearrange("p (c n) -> p c n", c=8)
        wp = ZF.bf16(2 * 1024).rearrange("p (c n) -> p c n", c=2)
        pTb = ZF.bf16(2 * 2048).rearrange("p (c n) -> p c n", c=2)
        sqf = [ZF.bf16(8 * 512).rearrange("p (c n) -> p c n", c=8) for _ in range(2)]
        D_sx = [ZF.f32(512) for _ in range(2)]
        rp = ZF.f32(T)
        sgm = [ZF.f32(512) for _ in range(2)]
        pj = [ZF.f32(512) for _ in range(2)]
        OWN = [(2 + 512 * q, 512) for q in range(4)]
        P.add("pool", lambda e: e.dma_start(out=wg, in_=w_gate[:, :, :]), writes=["wg"], dma=dsem("wg"))
        P.add("pool", lambda e: e.dma_start(out=wp, in_=w_proj[:, :, :]), writes=["wp"], dma=dsem("wp"))
        P.add("pool", lambda e: e.dma_start(out=pTb, in_=pT_own[:, :, :]), writes=["pTb"], dma=dsem("pTb"))
        rmsnorm_to(x1, hf, V_GPLE, OWN, rp, "x2", sqf, D_sx)
        for d in range(8):
            for q, (c0, n) in enumerate(OWN):
                s = (d * 4 + q) % 2
                b1, b2 = nextbank(), nextbank()

                def peg(e, d=d, q=q, c0=c0, b1=b1, b2=b2):
                    ins = None
                    for c in range(8):
                        ins = e.matmul(bank(b1), lhsT=wg[:, c, d * 128:(d + 1) * 128], rhs=hf[:, c, c0:c0 + 512], start=(c == 0), stop=(c == 7))
                    for kc in range(2):
                        ins = e.matmul(bank(b2), lhsT=wp[:, kc, d * 128:(d + 1) * 128], rhs=pTb[:, kc, q * 512:(q + 1) * 512], start=(kc == 0), stop=(kc == 1))
                    return ins
                P.add("pe", peg, reads=["wg", "wp", "pTb"] + [("hf", dd, q) for dd in range(8)], writes=[("ps", b1), ("ps", b2)])
                P.add("act", lambda e, s=s, b1=b1: e.activation(out=sgm[s], in_=bank(b1), func=AF.Sigmoid), reads=[("ps", b1)], writes=[("sgm", s)])
                P.add("dve", lambda e, s=s, b2=b2: e.tensor_tensor(out=pj[s], in0=bank(b2), in1=sgm[s], op=ALU.mult), reads=[("ps", b2), ("sgm", s)], writes=[("pj", s)])
                P.add("dve", lambda e, s=s, d=d, c0=c0: e.tensor_tensor(out=x1[:, d, c0:c0 + 512], in0=x1[:, d, c0:c0 + 512], in1=pj[s], op=ALU.add),
                      reads=[("pj", s)], writes=[("x3", d, q)])
        P.barrier()
        rmsnorm_to(x1, None, 0, OWN, rp, "x3", sqf, D_sx)
        for q, (c0, n) in enumerate(OWN):
            for hh in range(2):
                for dd in range(4):
                    d = hh * 4 + dd
                    P.add("dve", lambda e, d=d, dd=dd, c0=c0: e.scalar_tensor_tensor(out=stg[0][:, dd, :], in0=x1[:, d, c0:c0 + 512], scalar=V(V_GFIN + d),
                                                                                     in1=rp[:, c0:c0 + 512], op0=ALU.mult, op1=ALU.mult),
                          reads=[("rbuf", q), "vec"], writes=[("stg", dd)])
                P.add("sp", lambda e, q=q, hh=hh: e.dma_start(out=outT[:, hh * 4:(hh + 1) * 4, q * 512:(q + 1) * 512], in_=stg[0]),
                      reads=[("stg", dd) for dd in range(4)], dma=dsem("out"))
        P.barrier()
        return finish(nc, P, dma_ids)


V_EPS = EPS


def finish(nc, P, dma_ids):
    from contextlib import ExitStack
    with ExitStack() as es:
        sems = {}
        for e in ("pe", "act", "dve", "pool"):
            sems[e] = es.enter_context(nc.semaphore("s_" + e))
        for _, i in dma_ids.items():
            sems[("d", i)] = es.enter_context(nc.semaphore("d%d" % i))
        block = es.enter_context(nc.Block())

        def emit(ename):
            def body(h):
                for waits, fn, inc in P.ops[ename]:
                    for sk, val in waits:
                        h.wait_ge(sems[sk], val)
                    if fn is not None:
                        fn(h).then_inc(sems[inc[0]], inc[1])
            return body
        block.tensor(emit("pe"))
        block.scalar(emit("act"))
        block.vector(emit("dve"))
        block.gpsimd(emit("pool"))
        block.sync(emit("sp"))
    return nc


def rope_tables():
    pos = np.arange(S, dtype=np.float32)
    inv_freq = (np.float32(10000.0) ** (-np.arange(0, 32, 2, dtype=np.float32) / np.float32(32))).astype(np.float32)
    ang = (pos[:, None] * inv_freq[None, :]).astype(np.float32)
    ang = np.concatenate([ang, ang], axis=-1)
    return np.cos(ang).astype(np.float32), np.sin(ang).astype(np.float32)


def chunked(w):
    K, N = w.shape
    return np.ascontiguousarray(w.reshape(K // 128, 128, N).transpose(1, 0, 2))


def make_in_maps(x, p, norm_mix_g, w_in, conv_w, q_norm_g, w_uq, kv_norm_g, w_ukv, w_o,
                 norm_ffn_g, w_up, ffn_conv_w, ffn_conv_b, w_down, ple_norm_g,
                 w_ple_gate, w_ple_proj, final_norm_g, cores=range(8)):
    f = lambda a: np.asarray(a, dtype=np.float32)
    x, p = f(x), f(p)
    cos, sin = rope_tables()
    shared = {
        "w_in": chunked(f(w_in)[0]), "w_uq": chunked(f(w_uq)[0]), "w_ukv": np.ascontiguousarray(f(w_ukv)[0]),
        "w_o": chunked(f(w_o)[0]), "w_gate": chunked(f(w_ple_gate)[0]), "w_proj": chunked(f(w_ple_proj)[0]),
        "w_down": np.ascontiguousarray(f(w_down)[0].reshape(NFF, 128, 1024)),
        "cosk": np.ascontiguousarray(cos.T), "sink": np.ascontiguousarray(sin.T),
    }
    wu = chunked(f(w_up)[0])
    wu = np.stack([np.concatenate([wu[:, :, j * 128:(j + 1) * 128], wu[:, :, 2816 + j * 128:2816 + (j + 1) * 128]], axis=2)
                   for j in range(NFF)], axis=0)
    shared["w_up"] = np.ascontiguousarray(wu)
    vec = np.zeros((128, NV), np.float32)
    col = lambda v: f(v).reshape(-1, 128).T
    vec[:, V_GMIX:V_GMIX + 8] = col(norm_mix_g[0])
    vec[:, V_GFFN:V_GFFN + 8] = col(norm_ffn_g[0])
    vec[:, V_GPLE:V_GPLE + 8] = col(ple_norm_g[0])
    vec[:, V_GFIN:V_GFIN + 8] = col(final_norm_g)
    vec[:, V_GQ:V_GQ + 2] = col(q_norm_g[0])
    vec[:, V_GKV:V_GKV + 1] = col(kv_norm_g[0])
    cw = f(conv_w)[0]
    for c in range(4):
        for k in range(3):
            vec[:, V_CONVW + 3 * c + k] = cw[k, c * 128:(c + 1) * 128]
    fw, fb = f(ffn_conv_w)[0], f(ffn_conv_b)[0]
    for j in range(2 * NFF):
        for k in range(3):
            vec[:, V_FFNW + 3 * j + k] = fw[k, j * 128:(j + 1) * 128]
        vec[:, V_FFNB + j] = fb[j * 128:(j + 1) * 128]
    maps = []
    for core in cores:
        b, c = core // 4, core % 4
        s0 = c * 2048
        xb = x[b]
        xp = np.zeros((S + 4, 1024), np.float32)
        xp[2:S + 2] = xb
        own = xp[s0:s0 + T]
        posq = np.clip(np.arange(s0 - 2, s0 - 2 + T), 0, S - 1)
        v = vec.copy()
        v[:, V_HM] = 1.0 if c > 0 else 0.0
        v[:, V_HM + 1] = 1.0 if c < 3 else 0.0
        m = dict(shared)
        m["xT_full"] = chunked(np.ascontiguousarray(xb.T))
        m["xT_own"] = chunked(np.ascontiguousarray(own.T))
        m["pT_own"] = chunked(np.ascontiguousarray(p[0, b, s0:s0 + 2048].T))
        m["vecs"] = v
        m["cosq"] = np.ascontiguousarray(cos[posq].T)
        m["sinq"] = np.ascontiguousarray(sin[posq].T)
        maps.append(m)
    return maps


_NC_CACHE = {}


def kernel(**inputs):
    if "nc" not in _NC_CACHE:
        _NC_CACHE["nc"] = build_program()
    nc = _NC_CACHE["nc"]
    maps = make_in_maps(**inputs)
    res = run_bass_kernel_spmd(nc, maps, core_ids=list(range(8)))
    out = np.empty((2, S, 1024), np.float32)
    for core in range(8):
        b, c = core // 4, core % 4
        o = res.results[core]["outT"]
        out[b, c * 2048:(c + 1) * 2048] = o.transpose(2, 1, 0).reshape(2048, 1024)
    return out
```

```python
import numpy as np
import ml_dtypes
import concourse.bass as bass
import concourse.mybir as mybir
from concourse.bass_utils import run_bass_kernel_spmd

F32 = mybir.dt.float32
BF16 = mybir.dt.bfloat16
AF = mybir.ActivationFunctionType
ALU = mybir.AluOpType

S = 8192
T = 2052
NH = 8
EPS = 1e-6
SCALE = 96.0 ** -0.5
NFF = 22
TILES = [(0, 512), (512, 512), (1024, 512), (1536, 512), (2048, 4)]
FT = [(1 + 510 * i, 512) for i in range(4)] + [(2041, 10)]

V_GMIX, V_GFFN, V_GPLE, V_GFIN, V_GQ, V_GKV, V_CONVW, V_FFNW, V_FFNB, V_HM, NV = (
    0, 8, 16, 24, 32, 34, 35, 47, 179, 223, 225)

ENGS = ("pe", "act", "dve", "pool", "sp")


class _Rec:
    def __init__(self):
        self.cost = 0.0
        self.dma_ns = 0.0

    def __getattr__(self, name):
        def f(*a, **kw):
            out = kw.get("out", a[0] if a else None)
            n = 1
            try:
                for d in out.shape[1:]:
                    n *= d
            except Exception:
                n = 512
            if name == "matmul":
                self.cost += n / 2.4 + 4
            elif name == "activation":
                self.cost += (n + 224) / 1.2
            elif name == "dma_start":
                self.cost += 60
                self.dma_ns += 2000 + n * out.shape[0] * 4 / 250.0
            elif name == "reciprocal":
                self.cost += 4.3 * n + 60
            elif name == "then_inc":
                pass
            else:
                self.cost += n / 0.96 + 60
            return self
        return f


class Op:
    __slots__ = ("eng", "fn", "deps", "idx", "sk", "amt", "cost", "lat", "val", "nusers", "sched")


class Prog:
    WINDOW = 48
    SYNC = 150.0

    def __init__(self, schedule=True):
        self.final = {e: [] for e in ENGS}
        self.cnt = {}
        self.seen = {e: {} for e in ENGS}
        self.res = {}
        self.region = []
        self.dirty = set()
        self.schedule = schedule
        self.nops = 0

    def add(self, eng, fn, reads=(), writes=(), dma=None):
        op = Op()
        op.eng, op.fn, op.idx = eng, fn, self.nops
        self.nops += 1
        deps = set()
        for k in reads:
            r = self.res.get(k)
            if r and r[0] is not None:
                deps.add(r[0])
        for k in writes:
            r = self.res.get(k)
            if r:
                if r[0] is not None:
                    deps.add(r[0])
                deps.update(r[1])
        op.deps = deps
        if dma is not None:
            op.sk, op.amt = ("d", dma), 16
        else:
            op.sk, op.amt = eng, 1
        rec = _Rec()
        fn(rec)
        op.cost = rec.cost
        op.lat = rec.cost + rec.dma_ns
        op.val = None
        op.sched = False
        for k in reads:
            self.res.setdefault(k, [None, []])[1].append(op)
        for k in writes:
            self.res[k] = [op, []]
        self.region.append(op)
        return op

    def _flush(self):
        ops = self.region
        self.region = []
        if not ops:
            return
        order = {e: [] for e in ENGS}
        if not self.schedule:
            for op in ops:
                order[op.eng].append(op)
        else:
            inreg = set(ops)
            pend = {e: [op for op in ops if op.eng == e] for e in ENGS}
            tfree = {e: 0.0 for e in ENGS}
            fin = {}
            remaining = len(ops)
            while remaining:
                best = None
                for e in ENGS:
                    lst = pend[e]
                    if not lst:
                        continue
                    cand = None
                    for op in lst[:self.WINDOW]:
                        ok = True
                        st = tfree[e]
                        for d in op.deps:
                            if d in inreg:
                                if not d.sched:
                                    ok = False
                                    break
                                t = fin[d] + (0.0 if (d.eng == e and e == "pe") else self.SYNC)
                                if t > st:
                                    st = t
                        if not ok:
                            continue
                        if cand is None or st < cand[0] - 1e-9:
                            cand = (st, op)
                        if st <= tfree[e] + 1e-9:
                            break
                    if cand is not None and (best is None or cand[0] < best[0] - 1e-9):
                        best = (cand[0], e, cand[1])
                assert best is not None, "scheduler deadlock"
                st, e, op = best
                op.sched = True
                tfree[e] = st + op.cost
                fin[op] = st + op.lat
                pend[e].remove(op)
                order[e].append(op)
                remaining -= 1
        for e in ENGS:
            for op in order[e]:
                v = self.cnt.get(op.sk, 0) + op.amt
                self.cnt[op.sk] = v
                op.val = v
                self.dirty.add(op.sk)
        for e in ENGS:
            for op in order[e]:
                waits = []
                for d in sorted(op.deps, key=lambda o: o.idx):
                    if d.eng == "pe" and e == "pe" and d.sk == "pe":
                        continue
                    assert d.val is not None
                    if self.seen[e].get(d.sk, 0) >= d.val:
                        continue
                    self.seen[e][d.sk] = d.val
                    waits.append((d.sk, d.val))
                self.final[e].append((waits, op.fn, (op.sk, op.amt)))

    def barrier(self):
        self._flush()
        for e in ENGS:
            waits = []
            for sk in sorted(self.dirty, key=str):
                if sk == e:
                    continue
                v = self.cnt[sk]
                if self.seen[e].get(sk, 0) >= v:
                    continue
                self.seen[e][sk] = v
                waits.append((sk, v))
            self.final[e].append((waits, None, None))
        self.res = {}
        self.dirty = set()

    @property
    def ops(self):
        return self.final


def build_program(dbg=None):
    nc = bass.Bass("TRN2", target_bir_lowering=False)

    def din(name, shape):
        return nc.dram_tensor(name, shape, F32, kind="ExternalInput").ap()

    xT_full = din("xT_full", [128, 8, S])
    xT_own = din("xT_own", [128, 8, T])
    pT_own = din("pT_own", [128, 2, 2048])
    w_in = din("w_in", [128, 8, 1952])
    w_uq = din("w_uq", [128, 2, 768])
    w_ukv = din("w_ukv", [128, 1024])
    w_o = din("w_o", [128, 8, 1024])
    w_up = din("w_up", [NFF, 128, 8, 256])
    w_down = din("w_down", [NFF, 128, 1024])
    w_gate = din("w_gate", [128, 8, 1024])
    w_proj = din("w_proj", [128, 2, 1024])
    vecs_d = din("vecs", [128, NV])
    cosk = din("cosk", [32, S])
    sink = din("sink", [32, S])
    cosq = din("cosq", [32, T])
    sinq = din("sinq", [32, T])
    outT = nc.dram_tensor("outT", [128, 8, 2048], F32, kind="ExternalOutput").ap()
    dbg_out = None
    if dbg is not None:
        dbg_out = nc.dram_tensor("dbg", [128, dbg[1]], F32, kind="ExternalOutput").ap()

    P = Prog()
    AW = 52000
    dma_ids = {}

    def dsem(name):
        if name not in dma_ids:
            dma_ids[name] = len(dma_ids)
        return dma_ids[name]

    with nc.sbuf_tensor("arena", [128, AW], F32) as arena_t, \
            nc.psum_tensor("ps", [128, 4096], F32) as ps_t:
        arena = arena_t[:]
        ps = ps_t[:]

        def bank(i, n=512, p0=0, p1=128, nb=1):
            return ps[p0:p1, i * 512:i * 512 + (n if nb == 1 else nb * 512)]

        class Zone:
            def __init__(self, base, size):
                self.base, self.size, self.off = base, size, 0

            def reset(self):
                self.off = 0

            def f32(self, n):
                n2 = (n + 1) // 2 * 2
                assert self.off + n2 <= self.size, ("arena overflow", self.off, n2, self.size)
                ap = arena[:, self.base + self.off:self.base + self.off + n]
                self.off += n2
                return ap

            def bf16(self, n):
                w = (n + 3) // 4 * 2
                assert self.off + w <= self.size, ("arena overflow", self.off, w, self.size)
                ap = arena[:, self.base + self.off:self.base + self.off + w].bitcast(BF16)[:, 0:n]
                self.off += w
                return ap

        Z0 = Zone(0, 2400)
        Z1 = Zone(2400, 10300)
        Z2 = Zone(12700, 8208)
        Z3 = Zone(20908, AW - 20908)

        vec = Z0.f32(NV)
        ones = Z0.bf16(128)
        wuq = Z0.bf16(2 * 768).rearrange("p (c n) -> p c n", c=2)
        wuqr = Z0.bf16(2 * 8 * 96).rearrange("p (c h n) -> p c h n", c=2, h=8)
        wukv = Z0.bf16(1024)
        kvn = Z1.bf16(S)
        krope = Z1.bf16(S)
        qn = Z1.bf16(2 * T).rearrange("p (c n) -> p c n", c=2)
        yT = Z2.bf16(8 * T).rearrange("p (c n) -> p c n", c=8)

        def V(c0, n=1):
            return vec[:, c0:c0 + n]

        P.add("sp", lambda e: e.dma_start(out=vec, in_=vecs_d[:, :]), writes=["vec"], dma=dsem("vec"))
        P.add("pool", lambda e: e.dma_start(out=wuq, in_=w_uq[:, :, :]), writes=["wuq"], dma=dsem("wuq"))
        P.add("pool", lambda e: e.dma_start(out=wukv, in_=w_ukv[:, :]), writes=["wukv"], dma=dsem("wukv"))
        P.add("dve", lambda e: e.memset(ones, 1.0), writes=["ones"])
        P.add("dve", lambda e: e.memset(wuqr, 0.0), writes=["wuqr"])
        wuq4 = wuq.rearrange("p c (h n) -> p c h n", h=8)
        P.add("dve", lambda e: e.tensor_scalar(out=wuqr[:, :, :, 64:80], in0=wuq4[:, :, :, 80:96], scalar1=-1.0,
                                               scalar2=None, op0=ALU.mult), reads=["wuq"], writes=["wuqr"])
        P.add("dve", lambda e: e.tensor_copy(out=wuqr[:, :, :, 80:96], in_=wuq4[:, :, :, 64:80]),
              reads=["wuq"], writes=["wuqr"])
        P.add("pool", lambda e: e.memset(yT[:, :, 0:1], 0.0), writes=["yT0"])
        P.add("pool", lambda e: e.memset(yT[:, :, T - 1:T], 0.0), writes=["yT0"])

        Z3.reset()
        wA = Z3.bf16(8 * 320).rearrange("p (c n) -> p c n", c=8)
        A_xt = [Z3.f32(8 * 512).rearrange("p (c n) -> p c n", c=8) for _ in range(2)]
        A_sq = [Z3.bf16(8 * 512).rearrange("p (c n) -> p c n", c=8) for _ in range(2)]
        A_xg = [Z3.bf16(8 * 512).rearrange("p (c n) -> p c n", c=8) for _ in range(2)]
        A_t = [{k: Z3.f32(512) for k in ("sx", "rstd", "kvl", "sk", "rk", "ca", "cb", "cs", "sn")} for _ in range(2)]
        A_sqk = [Z3.bf16(512) for _ in range(2)]

        P.add("pool", lambda e: e.dma_start(out=wA[:, :, 0:128], in_=w_in[:, :, 1792:1920]), writes=["wA"], dma=dsem("wA"))
        P.add("pool", lambda e: e.dma_start(out=wA[:, :, 192:224], in_=w_in[:, :, 1920:1952]), writes=["wA1"], dma=dsem("wA1"))
        P.add("dve", lambda e: e.memset(wA[:, :, 128:192], 0.0), writes=["wAz"])
        P.add("dve", lambda e: e.memset(wA[:, :, 224:288], 0.0), writes=["wAz"])
        P.add("dve", lambda e: e.tensor_scalar(out=wA[:, :, 288:304], in0=wA[:, :, 208:224], scalar1=-1.0, scalar2=None,
                                               op0=ALU.mult), reads=["wA1"], writes=["wAr"])
        P.add("dve", lambda e: e.tensor_copy(out=wA[:, :, 304:320], in_=wA[:, :, 192:208]), reads=["wA1"], writes=["wAr2"])
        gmix_b = V(V_GMIX, 8).unsqueeze(2).to_broadcast([128, 8, 512])

        def A_front(i):
            s = i % 2
            c0 = i * 512
            tt = A_t[s]
            P.add("sp", lambda e, s=s, c0=c0: e.dma_start(out=A_xt[s], in_=xT_full[:, :, c0:c0 + 512]),
                  writes=[("Axt", s)], dma=dsem(("Axt", s)))
            P.add("sp", lambda e, tt=tt, c0=c0: e.dma_start(out=tt["cs"][64:96, :], in_=cosk[:, c0:c0 + 512]),
                  writes=[("Acs", s)], dma=dsem(("Acs", s)))
            P.add("sp", lambda e, tt=tt, c0=c0: e.dma_start(out=tt["sn"][64:96, :], in_=sink[:, c0:c0 + 512]),
                  writes=[("Asn", s)], dma=dsem(("Asn", s)))
            P.add("act", lambda e, s=s: e.activation(out=A_sq[s], in_=A_xt[s], func=AF.Square),
                  reads=[("Axt", s)], writes=[("Asq", s)])
            P.add("dve", lambda e, s=s: e.tensor_tensor(out=A_xg[s], in0=A_xt[s], in1=gmix_b, op=ALU.mult),
                  reads=[("Axt", s), "vec"], writes=[("Axg", s)])

            def pe1(e, s=s):
                ins = None
                for c in range(8):
                    ins = e.matmul(bank(4 * s), lhsT=ones, rhs=A_sq[s][:, c, :], start=(c == 0), stop=(c == 7))
                return ins
            P.add("pe", pe1, reads=[("Asq", s), "ones"], writes=[("ps", 4 * s)])

            def pe2(e, s=s):
                ins = None
                for c in range(8):
                    ins = e.matmul(bank(4 * s + 1), lhsT=wA[:, c, 0:128], rhs=A_xg[s][:, c, :], start=(c == 0), stop=(c == 7))
                for c in range(8):
                    ins = e.matmul(bank(4 * s + 2, p1=96), lhsT=wA[:, c, 128:224], rhs=A_xg[s][:, c, :], start=(c == 0), stop=(c == 7))
                for c in range(8):
                    ins = e.matmul(bank(4 * s + 3, p1=96), lhsT=wA[:, c, 224:320], rhs=A_xg[s][:, c, :], start=(c == 0), stop=(c == 7))
                return ins
            P.add("pe", pe2, reads=[("Axg", s), "wA", "wA1", "wAz", "wAr", "wAr2"],
                  writes=[("ps", 4 * s + 1), ("ps", 4 * s + 2), ("ps", 4 * s + 3)])

        def A_back(i):
            s = i % 2
            c0 = i * 512
            tt = A_t[s]
            P.add("act", lambda e, s=s, tt=tt: e.activation(out=tt["sx"], in_=bank(4 * s), func=AF.Ln, scale=1.0 / 1024, bias=V_EPS),
                  reads=[("ps", 4 * s), "eps"], writes=[("Asx", s)])
            P.add("act", lambda e, tt=tt: e.activation(out=tt["rstd"], in_=tt["sx"], func=AF.Exp, scale=-0.5), reads=[("Asx", s)], writes=[("Arstd", s)])
            P.add("dve", lambda e, s=s, tt=tt: e.tensor_tensor(out=tt["kvl"], in0=bank(4 * s + 1), in1=tt["rstd"], op=ALU.mult),
                  reads=[("ps", 4 * s + 1), ("Arstd", s)], writes=[("Akvl", s)])
            P.add("act", lambda e, s=s, tt=tt: e.activation(out=A_sqk[s], in_=tt["kvl"], func=AF.Square),
                  reads=[("Akvl", s)], writes=[("Asqk", s)])
            P.add("pe", lambda e, s=s: e.matmul(bank(4 * s), lhsT=ones, rhs=A_sqk[s], start=True, stop=True),
                  reads=[("Asqk", s), "ones"], writes=[("ps", 4 * s)])
            P.add("act", lambda e, s=s, tt=tt: e.activation(out=tt["sk"], in_=bank(4 * s), func=AF.Ln, scale=1.0 / 128, bias=V_EPS),
                  reads=[("ps", 4 * s), "eps"], writes=[("Ask", s)])
            P.add("act", lambda e, tt=tt: e.activation(out=tt["rk"], in_=tt["sk"], func=AF.Exp, scale=-0.5), reads=[("Ask", s)], writes=[("Ark", s)])
            P.add("dve", lambda e, tt=tt, c0=c0: e.scalar_tensor_tensor(out=kvn[:, c0:c0 + 512], in0=tt["kvl"], scalar=V(V_GKV),
                                                                         in1=tt["rk"], op0=ALU.mult, op1=ALU.mult),
                  reads=[("Akvl", s), ("Ark", s), "vec"], writes=[("kvn", i)])
            P.add("dve", lambda e, s=s, tt=tt: e.tensor_tensor(out=tt["ca"][64:96, :], in0=bank(4 * s + 2, p0=64, p1=96),
                                                               in1=tt["cs"][64:96, :], op=ALU.mult),
                  reads=[("ps", 4 * s + 2), ("Acs", s)], writes=[("Aca", s)])
            P.add("dve", lambda e, s=s, tt=tt: e.tensor_tensor(out=tt["cb"][64:96, :], in0=bank(4 * s + 3, p0=64, p1=96),
                                                               in1=tt["sn"][64:96, :], op=ALU.mult),
                  reads=[("ps", 4 * s + 3), ("Asn", s)], writes=[("Acb", s)])
            P.add("dve", lambda e, tt=tt: e.tensor_tensor(out=tt["ca"][64:96, :], in0=tt["ca"][64:96, :], in1=tt["cb"][64:96, :], op=ALU.add),
                  reads=[("Aca", s), ("Acb", s)], writes=[("Aca", s)])
            P.add("dve", lambda e, tt=tt, c0=c0: e.tensor_tensor(out=krope[64:96, c0:c0 + 512], in0=tt["ca"][64:96, :], in1=tt["rstd"][64:96, :], op=ALU.mult),
                  reads=[("Aca", s), ("Arstd", s)], writes=[("krope", i)])

        for i in range(17):
            if i < 16:
                A_front(i)
            if i >= 1:
                A_back(i - 1)

        P.barrier()
        if dbg and dbg[0] == "A":
            P.add("sp", lambda e: e.dma_start(out=dbg_out[:, 0:S].bitcast(BF16)[:, 0:S], in_=kvn), dma=dsem("dbg"))
            P.add("sp", lambda e: e.dma_start(out=dbg_out[:, S:2 * S].bitcast(BF16)[:, 0:S], in_=krope), dma=dsem("dbg"))
            P.barrier()
            return finish(nc, P, dma_ids)

        Z3.reset()
        xgT = Z3.bf16(8 * T).rearrange("p (c n) -> p c n", c=8)
        rstd1 = Z3.f32(T)
        mark = Z3.off
        B_xt = [Z3.f32(8 * 512).rearrange("p (c n) -> p c n", c=8) for _ in range(2)]
        B_sq = [Z3.bf16(8 * 512).rearrange("p (c n) -> p c n", c=8) for _ in range(2)]
        B_sx = [Z3.f32(512) for _ in range(2)]
        for ti, (c0, n) in enumerate(TILES):
            s = ti % 2
            P.add("sp", lambda e, s=s, c0=c0, n=n: e.dma_start(out=B_xt[s][:, :, 0:n], in_=xT_own[:, :, c0:c0 + n]),
                  writes=[("Bxt", s)], dma=dsem(("Bxt", s)))
            P.add("act", lambda e, s=s, n=n: e.activation(out=B_sq[s][:, :, 0:n], in_=B_xt[s][:, :, 0:n], func=AF.Square),
                  reads=[("Bxt", s)], writes=[("Bsq", s)])
            P.add("dve", lambda e, s=s, c0=c0, n=n: e.tensor_tensor(out=xgT[:, :, c0:c0 + n], in0=B_xt[s][:, :, 0:n],
                                                                    in1=V(V_GMIX, 8).unsqueeze(2).to_broadcast([128, 8, n]), op=ALU.mult),
                  reads=[("Bxt", s), "vec"], writes=[("xgT", ti)])

            def pe1(e, s=s, n=n):
                ins = None
                for c in range(8):
                    ins = e.matmul(bank(s, n), lhsT=ones, rhs=B_sq[s][:, c, 0:n], start=(c == 0), stop=(c == 7))
                return ins
            P.add("pe", pe1, reads=[("Bsq", s), "ones"], writes=[("ps", s)])
            P.add("act", lambda e, s=s, n=n: e.activation(out=B_sx[s][:, 0:n], in_=bank(s, n), func=AF.Ln, scale=1.0 / 1024, bias=V_EPS),
                  reads=[("ps", s), "eps"], writes=[("Bsx", s)])
            P.add("act", lambda e, s=s, c0=c0, n=n: e.activation(out=rstd1[:, c0:c0 + n], in_=B_sx[s][:, 0:n], func=AF.Exp, scale=-0.5),
                  reads=[("Bsx", s)], writes=[("rstd1", ti)])
        P.barrier()
        Z3.off = mark
        wS = [Z3.bf16(8 * 384).rearrange("p (c n) -> p c n", c=8) for _ in range(2)]
        wQ = Z3.bf16(8 * 256).rearrange("p (c n) -> p c n", c=8)
        B_xs = [Z3.f32(512) for _ in range(2)]
        B_t1 = [Z3.f32(512) for _ in range(2)]
        B_u = Z3.f32(T)
        rstd2 = Z3.f32(T)
        P.add("dve", lambda e: e.tensor_tensor(out=rstd2, in0=rstd1, in1=rstd1, op=ALU.mult), writes=["rstd2"])
        B_bg = Z3.f32(T)
        B_c = Z3.f32(T)
        ql = Z3.f32(2 * T).rearrange("p (c n) -> p c n", c=2)
        B_sqq = [Z3.bf16(512) for _ in range(2)]
        B_sxq = [Z3.f32(512)] * 2
        B_rq = [Z3.f32(512)] * 2
        P.add("pool", lambda e: e.dma_start(out=wQ, in_=w_in[:, :, 1536:1792]), writes=["wQ"], dma=dsem("wQ"))
        pb = [0]

        def nextbank():
            pb[0] = (pb[0] + 1) % 8
            return pb[0]

        for k in range(4):
            ws = k % 2
            for j, src in enumerate((k * 128, 1024 + k * 128, 512 + k * 128)):
                P.add("pool", lambda e, ws=ws, j=j, src=src: e.dma_start(out=wS[ws][:, :, j * 128:(j + 1) * 128], in_=w_in[:, :, src:src + 128]),
                      writes=[("wS", ws, j)], dma=dsem(("wS", ws, j)))
            for ti, (c0, n) in enumerate(TILES):
                s = ti % 2
                bx, bc, bb = nextbank(), nextbank(), nextbank()

                def pe3(e, ws=ws, c0=c0, n=n, bx=bx, bc=bc, bb=bb):
                    ins = None
                    for j, b in enumerate((bx, bc, bb)):
                        for c in range(8):
                            ins = e.matmul(bank(b, n), lhsT=wS[ws][:, c, j * 128:(j + 1) * 128], rhs=xgT[:, c, c0:c0 + n],
                                           start=(c == 0), stop=(c == 7))
                    return ins
                P.add("pe", pe3, reads=[("wS", ws, 0), ("wS", ws, 1), ("wS", ws, 2), ("xgT", ti)], writes=[("ps", bx), ("ps", bc), ("ps", bb)])
                P.add("act", lambda e, s=s, n=n, bx=bx: e.activation(out=B_xs[s][:, 0:n], in_=bank(bx, n), func=AF.Copy),
                      reads=[("ps", bx)], writes=[("Bxs", s)])
                P.add("dve", lambda e, s=s, c0=c0, n=n, bc=bc: e.tensor_tensor(out=B_t1[s][:, 0:n], in0=bank(bc, n), in1=B_xs[s][:, 0:n], op=ALU.mult),
                      reads=[("ps", bc), ("Bxs", s)], writes=[("Bt1", s)])
                P.add("dve", lambda e, s=s, c0=c0, n=n: e.tensor_tensor(out=B_u[:, c0:c0 + n], in0=B_t1[s][:, 0:n], in1=rstd2[:, c0:c0 + n], op=ALU.mult),
                      reads=[("Bt1", s), "rstd2"], writes=[("Bu", ti)])
                P.add("dve", lambda e, c0=c0, n=n, bb=bb: e.tensor_tensor(out=B_bg[:, c0:c0 + n], in0=bank(bb, n), in1=rstd1[:, c0:c0 + n], op=ALU.mult),
                      reads=[("ps", bb), ("rstd1", ti)], writes=[("Bbg", ti)])
            allu = [("Bu", ti) for ti in range(5)]
            allbg = [("Bbg", ti) for ti in range(5)]
            cw = V_CONVW + 3 * k
            P.add("dve", lambda e, cw=cw: e.tensor_scalar(out=B_c[:, 1:T - 1], in0=B_u[:, 1:T - 1], scalar1=V(cw + 1), scalar2=None, op0=ALU.mult),
                  reads=allu + ["vec"], writes=["Bc"])
            P.add("dve", lambda e, cw=cw: e.scalar_tensor_tensor(out=B_c[:, 1:T - 1], in0=B_u[:, 0:T - 2], scalar=V(cw), in1=B_c[:, 1:T - 1],
                                                                  op0=ALU.mult, op1=ALU.add), reads=allu + ["Bc"], writes=["Bc"])
            P.add("dve", lambda e, cw=cw: e.scalar_tensor_tensor(out=B_c[:, 1:T - 1], in0=B_u[:, 2:T], scalar=V(cw + 2), in1=B_c[:, 1:T - 1],
                                                                  op0=ALU.mult, op1=ALU.add), reads=allu + ["Bc"], writes=["Bc"])
            P.add("dve", lambda e, k=k: e.tensor_tensor(out=yT[:, k, 1:T - 1], in0=B_bg[:, 1:T - 1], in1=B_c[:, 1:T - 1], op=ALU.mult),
                  reads=allbg + ["Bc"], writes=[("yT", k)])
        for ti, (c0, n) in enumerate(TILES):
            s = ti % 2
            bq = nextbank()
            for kc in range(2):
                b = nextbank()

                def pe4(e, kc=kc, c0=c0, n=n, b=b):
                    ins = None
                    for c in range(8):
                        ins = e.matmul(bank(b, n), lhsT=wQ[:, c, kc * 128:(kc + 1) * 128], rhs=xgT[:, c, c0:c0 + n], start=(c == 0), stop=(c == 7))
                    return ins
                P.add("pe", pe4, reads=["wQ", ("xgT", ti)], writes=[("ps", b)])
                P.add("dve", lambda e, kc=kc, c0=c0, n=n, b=b: e.tensor_tensor(out=ql[:, kc, c0:c0 + n], in0=bank(b, n), in1=rstd1[:, c0:c0 + n], op=ALU.mult),
                      reads=[("ps", b), ("rstd1", ti)], writes=[("ql", ti, kc)])
                P.add("act", lambda e, s=s, kc=kc, c0=c0, n=n: e.activation(out=B_sqq[s][:, 0:n], in_=ql[:, kc, c0:c0 + n], func=AF.Square),
                      reads=[("ql", ti, kc)], writes=[("Bsqq", s)])
                P.add("pe", lambda e, s=s, kc=kc, n=n, bq=bq: e.matmul(bank(bq, n), lhsT=ones, rhs=B_sqq[s][:, 0:n], start=(kc == 0), stop=(kc == 1)),
                      reads=[("Bsqq", s), "ones"], writes=[("ps", bq)])
            P.add("act", lambda e, s=s, n=n, bq=bq: e.activation(out=B_sxq[s][:, 0:n], in_=bank(bq, n), func=AF.Ln, scale=1.0 / 256, bias=V_EPS),
                  reads=[("ps", bq), "eps"], writes=[("Bsxq", 0)])
            P.add("act", lambda e, s=s, n=n: e.activation(out=B_rq[s][:, 0:n], in_=B_sxq[s][:, 0:n], func=AF.Exp, scale=-0.5), reads=[("Bsxq", 0)], writes=[("Brq", 0)])
            for kc in range(2):
                P.add("dve", lambda e, s=s, kc=kc, c0=c0, n=n: e.scalar_tensor_tensor(out=qn[:, kc, c0:c0 + n], in0=ql[:, kc, c0:c0 + n], scalar=V(V_GQ + kc),
                                                                                      in1=B_rq[s][:, 0:n], op0=ALU.mult, op1=ALU.mult),
                      reads=[("ql", ti, kc), ("Brq", 0), "vec"], writes=[("qn", ti)])
        P.barrier()
        if dbg and dbg[0] == "B":
            P.add("sp", lambda e: e.dma_start(out=dbg_out[:, 0:4 * T].bitcast(BF16)[:, 0:8 * T].rearrange("p (c n) -> p c n", c=8), in_=yT), dma=dsem("dbg"))
            P.add("sp", lambda e: e.dma_start(out=dbg_out[:, 4 * T:5 * T].bitcast(BF16)[:, 0:2 * T].rearrange("p (c n) -> p c n", c=2), in_=qn), dma=dsem("dbg"))
            P.barrier()
            return finish(nc, P, dma_ids)

        Z3.reset()
        Kh = [Z3.bf16(S) for _ in range(2)]
        Vh = [Z3.bf16(64 * 128).rearrange("p (k d) -> p k d", d=128) for _ in range(2)]
        Qh = [Z3.bf16(T) for _ in range(2)]
        Pb = [Z3.bf16(1024) for _ in range(4)]
        cq = Z3.f32(T)
        sq_ = Z3.f32(T)
        R_t1 = [Z3.f32(512) for _ in range(2)]
        R_t2 = [Z3.f32(512) for _ in range(2)]
        rec = Z3.f32(1024)
        ocp = [Z3.f32(1024) for _ in range(2)]
        ocn = [0]
        Ph = Z3.bf16(128)
        rech = Z3.f32(2)
        P.add("sp", lambda e: e.dma_start(out=cq[64:96, :], in_=cosq[:, :]), writes=["cq"], dma=dsem("cq"))
        P.add("sp", lambda e: e.dma_start(out=sq_[64:96, :], in_=sinq[:, :]), writes=["sq"], dma=dsem("sq"))
        for s in range(2):
            P.add("pool", lambda e, s=s: e.memset(Vh[s][:, :, 64:128], 1.0), writes=[("Vones", s)])
        bb = [0]

        def bbank():
            bb[0] = 1 - bb[0]
            return 6 + bb[0]

        def build_steps(h):
            s = h % 2
            steps = []
            steps.append(lambda: P.add("sp", lambda e: e.dma_start(out=Kh[s][64:96, :], in_=krope[64:96, :]),
                                       reads=[("krope", i) for i in range(16)] if h < 2 else [], writes=[("Khr", s)], dma=dsem(("Khr", s))))
            for t in range(16):
                def st(t=t):
                    b = bbank()
                    P.add("pe", lambda e: e.matmul(bank(b, p1=64), lhsT=wukv[:, h * 128:h * 128 + 64], rhs=kvn[:, t * 512:(t + 1) * 512], start=True, stop=True),
                          reads=["wukv"] + ([("kvn", t)] if h < 2 else []), writes=[("ps", b)])
                    P.add("dve", lambda e: e.tensor_copy(out=Kh[s][0:64, t * 512:(t + 1) * 512], in_=bank(b, p1=64)),
                          reads=[("ps", b)], writes=[("Kh", s, t)])
                steps.append(st)
            for g in range(8):
                def st(g=g):
                    b = bbank()

                    def pev(e):
                        ins = None
                        for j in range(8):
                            kt = 8 * g + j
                            ins = e.matmul(bank(b)[:, j * 64:(j + 1) * 64], lhsT=kvn[:, kt * 128:(kt + 1) * 128], rhs=wukv[:, h * 128 + 64:h * 128 + 128],
                                           start=True, stop=True)
                        return ins
                    P.add("pe", pev, reads=["wukv"] + ([("kvn", t) for t in range(16)] if h < 2 else []), writes=[("ps", b)])
                    P.add("dve", lambda e: e.tensor_copy(out=Vh[s][:, 8 * g:8 * g + 8, 0:64], in_=bank(b).rearrange("p (j d) -> p j d", d=64)),
                          reads=[("ps", b)], writes=[("Vh", s, g)])
                steps.append(st)
            for ti, (c0, n) in enumerate(TILES):
                def st(ti=ti, c0=c0, n=n):
                    ba, bb_ = bbank(), bbank()
                    r = ti % 2

                    def peq(e):
                        ins = None
                        for kc in range(2):
                            ins = e.matmul(bank(ba, n, p1=96), lhsT=wuq[:, kc, h * 96:(h + 1) * 96], rhs=qn[:, kc, c0:c0 + n], start=(kc == 0), stop=(kc == 1))
                        for kc in range(2):
                            ins = e.matmul(bank(bb_, n, p1=96), lhsT=wuqr[:, kc, h, :], rhs=qn[:, kc, c0:c0 + n], start=(kc == 0), stop=(kc == 1))
                        return ins
                    P.add("pe", peq, reads=["wuq", "wuqr"] + ([("qn", ti)] if h < 2 else []), writes=[("ps", ba), ("ps", bb_)])
                    P.add("dve", lambda e: e.tensor_copy(out=Qh[s][0:64, c0:c0 + n], in_=bank(ba, n, p1=64)), reads=[("ps", ba)], writes=[("Qh", s, ti)])
                    P.add("dve", lambda e: e.tensor_tensor(out=R_t1[r][64:96, 0:n], in0=bank(ba, n, p0=64, p1=96), in1=cq[64:96, c0:c0 + n], op=ALU.mult),
                          reads=[("ps", ba), "cq"], writes=[("Rt1", r)])
                    P.add("dve", lambda e: e.tensor_tensor(out=R_t2[r][64:96, 0:n], in0=bank(bb_, n, p0=64, p1=96), in1=sq_[64:96, c0:c0 + n], op=ALU.mult),
                          reads=[("ps", bb_), "sq"], writes=[("Rt2", r)])
                    P.add("dve", lambda e: e.tensor_tensor(out=Qh[s][64:96, c0:c0 + n], in0=R_t1[r][64:96, 0:n], in1=R_t2[r][64:96, 0:n], op=ALU.add),
                          reads=[("Rt1", r), ("Rt2", r)], writes=[("Qh", s, ti)])
                steps.append(st)
            return steps

        def khv_reads(s):
            return ([("Khr", s)] + [("Kh", s, t) for t in range(16)] + [("Qh", s, ti) for ti in range(5)])

        def v_reads(s):
            return [("Vh", s, g) for g in range(8)] + [("Vones", s)]

        for st in build_steps(0):
            st()
        for h in range(NH):
            s = h % 2
            pending = build_steps(h + 1) if h + 1 < NH else []
            kreads, vreads = khv_reads(s), v_reads(s)
            dst_p0 = (h % 2) * 64
            ych = 4 + h // 2
            bS = bbank()

            def pehs(e, s=s, bS=bS):
                ins = None
                for kt in range(64):
                    ins = e.matmul(bank(bS)[:, 2 * kt:2 * kt + 2], lhsT=Kh[s][0:96, kt * 128:(kt + 1) * 128], rhs=Qh[s][0:96, 1:T - 1:T - 3], start=True, stop=True)
                return ins
            P.add("pe", pehs, reads=kreads, writes=[("ps", bS)])
            P.add("act", lambda e, bS=bS: e.activation(out=Ph, in_=bank(bS, 128), func=AF.Exp, scale=SCALE), reads=[("ps", bS)], writes=["Ph"])
            bO = bbank()

            def peho(e, s=s, bO=bO):
                ins = None
                for kt in range(64):
                    ins = e.matmul(bank(bO, 2), lhsT=Vh[s][:, kt, :], rhs=Ph[:, 2 * kt:2 * kt + 2], start=(kt == 0), stop=(kt == 63))
                return ins
            P.add("pe", peho, reads=vreads + ["Ph"], writes=[("ps", bO)])
            P.add("dve", lambda e, bO=bO: e.reciprocal(out=rech[0:64, :], in_=bank(bO, 2, p0=64, p1=128)), reads=[("ps", bO)], writes=["rech"])
            P.add("dve", lambda e, bO=bO, dst_p0=dst_p0, ych=ych: e.tensor_tensor(out=yT[dst_p0:dst_p0 + 64, ych, 1:T - 1:T - 3], in0=bank(bO, 2, p1=64),
                                                                                  in1=rech[0:64, :], op=ALU.mult),
                  reads=[("ps", bO), "rech"], writes=[("yTh", h)])
            seq = [(qh, kt) for qh in range(2) for kt in range(64)]
            nsteps = len(seq)

            def issue_S(idx, s=s):
                qh, kt = seq[idx]
                sb = idx % 2
                q0 = 2 + 1024 * qh

                def pes(e):
                    ins = None
                    for j in range(2):
                        ins = e.matmul(bank(2 * sb + j), lhsT=Kh[s][0:96, kt * 128:(kt + 1) * 128], rhs=Qh[s][0:96, q0 + 512 * j:q0 + 512 * (j + 1)],
                                       start=True, stop=True)
                    return ins
                P.add("pe", pes, reads=kreads, writes=[("ps", 2 * sb), ("ps", 2 * sb + 1)])

            def issue_E(idx, s=s):
                sb = idx % 2
                pi = idx % 4
                P.add("act", lambda e: e.activation(out=Pb[pi], in_=bank(2 * sb, nb=2), func=AF.Exp, scale=SCALE),
                      reads=[("ps", 2 * sb), ("ps", 2 * sb + 1)], writes=[("Pb", pi)])

            def issue_PV(idx, s=s, dst_p0=dst_p0, ych=ych, h=h):
                qh, kt = seq[idx]
                q0 = 2 + 1024 * qh
                pi = idx % 4

                def pepv(e):
                    ins = None
                    for j in range(2):
                        ins = e.matmul(bank(4 + j), lhsT=Vh[s][:, kt, :], rhs=Pb[pi][:, 512 * j:512 * (j + 1)], start=(kt == 0), stop=(kt == 63))
                    return ins
                P.add("pe", pepv, reads=vreads + [("Pb", pi)], writes=[("ps", 4), ("ps", 5)])
                if kt == 63:
                    oc = ocp[ocn[0] % 2]
                    ocn[0] += 1
                    P.add("dve", lambda e: e.tensor_copy(out=oc, in_=bank(4, nb=2)), reads=[("ps", 4), ("ps", 5)], writes=[("oc", id(oc))])
                    P.add("dve", lambda e: e.reciprocal(out=rec[0:64, :], in_=oc[64:128, :]), reads=[("oc", id(oc))], writes=["rec"])
                    P.add("dve", lambda e: e.tensor_tensor(out=yT[dst_p0:dst_p0 + 64, ych, q0:q0 + 1024], in0=oc[0:64, :], in1=rec[0:64, :], op=ALU.mult),
                          reads=[("oc", id(oc)), "rec"], writes=[("yTm", h, qh)])

            issue_S(0)
            for idx in range(nsteps):
                if idx + 1 < nsteps:
                    issue_S(idx + 1)
                issue_E(idx)
                if idx >= 1:
                    issue_PV(idx - 1)
                if pending and idx % 3 == 2:
                    pending.pop(0)()
            issue_PV(nsteps - 1)
            while pending:
                pending.pop(0)()
        P.barrier()
        if dbg and dbg[0] == "C":
            P.add("sp", lambda e: e.dma_start(out=dbg_out[:, 0:4 * T].bitcast(BF16)[:, 0:8 * T].rearrange("p (c n) -> p c n", c=8), in_=yT), dma=dsem("dbg"))
            P.barrier()
            return finish(nc, P, dma_ids)


        Z3.reset()
        x1 = Z3.f32(8 * T).rearrange("p (c n) -> p c n", c=8)
        hf = Z3.bf16(8 * T).rearrange("p (c n) -> p c n", c=8)
        wo_raw = Z3.bf16(8 * 1024)
        wo = wo_raw.rearrange("p (c n) -> p c n", c=8)
        stg = [Z3.f32(4 * 512).rearrange("p (c n) -> p c n", c=4) for _ in range(1)]
        Z1.reset()
        xr = [Z1.f32(T) for _ in range(2)]
        rf = Z1.f32(T)
        sqf = [Z1.bf16(8 * 512).rearrange("p (c n) -> p c n", c=8) for _ in range(2)]
        D_sx = [wo_raw.bitcast(F32)[:, 0:512], wo_raw.bitcast(F32)[:, 512:1024]]
        P.add("pool", lambda e: e.dma_start(out=wo, in_=w_o[:, :, :]), writes=["wo"], dma=dsem("wo"))
        for d in range(8):
            r = d % 2
            P.add("sp", lambda e, d=d, r=r: e.dma_start(out=xr[r], in_=xT_own[:, d, :]), writes=[("xr", r)], dma=dsem(("xr", r)))
            for ti, (c0, n) in enumerate(TILES):
                b = nextbank()

                def pewo(e, d=d, c0=c0, n=n, b=b):
                    ins = None
                    for c in range(8):
                        ins = e.matmul(bank(b, n), lhsT=wo[:, c, d * 128:(d + 1) * 128], rhs=yT[:, c, c0:c0 + n], start=(c == 0), stop=(c == 7))
                    return ins
                P.add("pe", pewo, reads=["wo"], writes=[("ps", b)])
                P.add("dve", lambda e, d=d, r=r, c0=c0, n=n, b=b: e.tensor_tensor(out=x1[:, d, c0:c0 + n], in0=bank(b, n), in1=xr[r][:, c0:c0 + n], op=ALU.add),
                      reads=[("ps", b), ("xr", r)], writes=[("x1", d, ti)])

        def rmsnorm_to(src, dst, gcol, tiles, rbuf, tag, sqf, D_sx):
            for ti, (c0, n) in enumerate(tiles):
                s = ti % 2
                bq = nextbank()
                P.add("act", lambda e, s=s, c0=c0, n=n: e.activation(out=sqf[s][:, :, 0:n], in_=src[:, :, c0:c0 + n], func=AF.Square),
                      reads=[(tag, d, "all") for d in range(8)] + [("x1", d, ti) for d in range(8)], writes=[("sqf", s)])

                def pen(e, s=s, n=n, bq=bq):
                    ins = None
                    for c in range(8):
                        ins = e.matmul(bank(bq, n), lhsT=ones, rhs=sqf[s][:, c, 0:n], start=(c == 0), stop=(c == 7))
                    return ins
                P.add("pe", pen, reads=[("sqf", s), "ones"], writes=[("ps", bq)])
                P.add("act", lambda e, s=s, n=n, bq=bq: e.activation(out=D_sx[s][:, 0:n], in_=bank(bq, n), func=AF.Ln, scale=1.0 / 1024, bias=V_EPS),
                      reads=[("ps", bq)], writes=[("Dsx", s), "wo"])
                P.add("act", lambda e, s=s, c0=c0, n=n: e.activation(out=rbuf[:, c0:c0 + n], in_=D_sx[s][:, 0:n], func=AF.Exp, scale=-0.5), reads=[("Dsx", s)], writes=[("rbuf", ti)])
                if dst is not None:
                    for d in range(8):
                        P.add("dve", lambda e, d=d, c0=c0, n=n: e.scalar_tensor_tensor(out=dst[:, d, c0:c0 + n], in0=src[:, d, c0:c0 + n], scalar=V(gcol + d),
                                                                                       in1=rbuf[:, c0:c0 + n], op0=ALU.mult, op1=ALU.mult),
                              reads=[("rbuf", ti), (tag, d, "all"), ("x1", d, ti), "vec"], writes=[("hf", d, ti)])

        rmsnorm_to(x1, hf, V_GFFN, TILES, rf, "x1", sqf, D_sx)
        P.add("dve", lambda e: e.tensor_scalar(out=hf[:, :, 1:2], in0=hf[:, :, 1:2], scalar1=V(V_HM), scalar2=None, op0=ALU.mult),
              reads=[("hf", d, 0) for d in range(8)] + ["vec"], writes=[("hf", d, 0) for d in range(8)])
        P.add("dve", lambda e: e.tensor_scalar(out=hf[:, :, T - 2:T - 1], in0=hf[:, :, T - 2:T - 1], scalar1=V(V_HM + 1), scalar2=None, op0=ALU.mult),
              reads=[("hf", d, 4) for d in range(8)] + ["vec"], writes=[("hf", d, 4) for d in range(8)])
        P.barrier()
        if dbg and dbg[0] == "D":
            P.add("sp", lambda e: e.dma_start(out=dbg_out[:, 0:8 * T].rearrange("p (c n) -> p c n", c=8), in_=x1), dma=dsem("dbg"))
            P.barrier()
            return finish(nc, P, dma_ids)

        ZF = Zone(2400, 18508)
        G = 4
        wup = [ZF.bf16(8 * 256).rearrange("p (c n) -> p c n", c=8) for _ in range(3)]
        wdn = [ZF.bf16(1024) for _ in range(8)]
        a0g = [ZF.f32(512) for _ in range(2)]
        a0u = [ZF.f32(512) for _ in range(2)]
        sgb = [ZF.f32(512) for _ in range(2)]
        actT = [[ZF.bf16(2048) for _ in range(G)] for _ in range(2)]
        hf_all = [("hf", d, ti) for d in range(8) for ti in range(5)]

        def load_w(j):
            P.add("pool", lambda e, j=j: e.dma_start(out=wup[j % 3], in_=w_up[j, :, :, :]), writes=[("wup", j % 3)], dma=dsem(("wup", j % 3)))
            P.add("pool", lambda e, j=j: e.dma_start(out=wdn[j % 8], in_=w_down[j, :, :]), writes=[("wdn", j % 8)], dma=dsem(("wdn", j % 8)))

        cnt2 = [0]

        def U(j):
            if j + 2 < NFF:
                load_w(j + 2)
            gs, jj = (j // G) % 2, j % G
            for fi, (c0, n) in enumerate(FT):
                s = cnt2[0] % 2
                cnt2[0] += 1
                bg_, bu_ = nextbank(), nextbank()

                def peu(e, c0=c0, n=n, bg_=bg_, bu_=bu_):
                    ins = None
                    for c in range(8):
                        ins = e.matmul(bank(bg_, n), lhsT=wup[j % 3][:, c, 0:128], rhs=hf[:, c, c0:c0 + n], start=(c == 0), stop=(c == 7))
                    for c in range(8):
                        ins = e.matmul(bank(bu_, n), lhsT=wup[j % 3][:, c, 128:256], rhs=hf[:, c, c0:c0 + n], start=(c == 0), stop=(c == 7))
                    return ins
                P.add("pe", peu, reads=[("wup", j % 3)] + hf_all, writes=[("ps", bg_), ("ps", bu_)])
                wg_, wu_ = V_FFNW + 3 * j, V_FFNW + 3 * (j + NFF)
                P.add("act", lambda e, s=s, n=n, bg_=bg_, wg_=wg_: e.activation(out=a0g[s][:, 0:n], in_=bank(bg_, n), func=AF.Identity, scale=V(wg_ + 1), bias=V(V_FFNB + j)),
                      reads=[("ps", bg_), "vec"], writes=[("a0g", s)])
                P.add("act", lambda e, s=s, n=n, bu_=bu_, wu_=wu_: e.activation(out=a0u[s][:, 0:n], in_=bank(bu_, n), func=AF.Identity, scale=V(wu_ + 1), bias=V(V_FFNB + j + NFF)),
                      reads=[("ps", bu_), "vec"], writes=[("a0u", s)])
                for (buf, key, bk, wv) in ((a0g, "a0g", bg_, wg_), (a0u, "a0u", bu_, wu_)):
                    P.add("dve", lambda e, s=s, n=n, buf=buf, bk=bk, wv=wv: e.scalar_tensor_tensor(out=buf[s][:, 1:n - 1], in0=bank(bk, n)[:, 0:n - 2], scalar=V(wv),
                                                                                                   in1=buf[s][:, 1:n - 1], op0=ALU.mult, op1=ALU.add),
                          reads=[("ps", bk), (key, s), "vec"], writes=[(key, s)])
                    P.add("dve", lambda e, s=s, n=n, buf=buf, bk=bk, wv=wv: e.scalar_tensor_tensor(out=buf[s][:, 1:n - 1], in0=bank(bk, n)[:, 2:n], scalar=V(wv + 2),
                                                                                                   in1=buf[s][:, 1:n - 1], op0=ALU.mult, op1=ALU.add),
                          reads=[("ps", bk), (key, s), "vec"], writes=[(key, s)])
                P.add("act", lambda e, s=s, n=n: e.activation(out=sgb[s][:, 0:n - 2], in_=a0g[s][:, 1:n - 1], func=AF.Silu),
                      reads=[("a0g", s)], writes=[("sgb", s)])
                o0 = c0 - 1
                P.add("dve", lambda e, s=s, n=n, o0=o0, gs=gs, jj=jj: e.tensor_tensor(out=actT[gs][jj][:, o0:o0 + n - 2], in0=sgb[s][:, 0:n - 2], in1=a0u[s][:, 1:n - 1], op=ALU.mult),
                      reads=[("sgb", s), ("a0u", s)], writes=[("actT", gs, jj, fi)])

        def Dn(g):
            gs = g % 2
            js = list(range(g * G, min((g + 1) * G, NFF)))
            areads = [("actT", gs, j % G, fi) for j in js for fi in range(5)] + [("wdn", j % 8) for j in js]
            for d in range(8):
                for q in range(4):
                    b = nextbank()

                    def ped(e, d=d, q=q, b=b):
                        ins = None
                        for ii, j in enumerate(js):
                            ins = e.matmul(bank(b), lhsT=wdn[j % 8][:, d * 128:(d + 1) * 128], rhs=actT[gs][j % G][:, q * 512:(q + 1) * 512],
                                           start=(ii == 0), stop=(ii == len(js) - 1))
                        return ins
                    P.add("pe", ped, reads=areads, writes=[("ps", b)])
                    P.add("dve", lambda e, d=d, q=q, b=b: e.tensor_tensor(out=x1[:, d, 2 + q * 512:2 + (q + 1) * 512], in0=bank(b), in1=x1[:, d, 2 + q * 512:2 + (q + 1) * 512], op=ALU.add),
                          reads=[("ps", b), ("x2", d, q)], writes=[("x2", d, q)])

        load_w(0)
        load_w(1)
        ngroups = (NFF + G - 1) // G
        for j in range(NFF):
            U(j)
            if j % G == 0 and j > 0:
                Dn(j // G - 1)
        Dn(ngroups - 1)
        P.barrier()
        if dbg and dbg[0] == "E":
            P.add("sp", lambda e: e.dma_start(out=dbg_out[:, 0:8 * T].rearrange("p (c n) -> p c n", c=8), in_=x1), dma=dsem("dbg"))
            P.barrier()
            return finish(nc, P, dma_ids)

        ZF.reset()
        wg = ZF.bf16(8 * 1024).rearrange("p (c n) -> p c n", c=8)
        wp = ZF.bf16(2 * 1024).rearrange("p (c n) -> p c n", c=2)
        pTb = ZF.bf16(2 * 2048).rearrange("p (c n) -> p c n", c=2)
        sqf = [ZF.bf16(8 * 512).rearrange("p (c n) -> p c n", c=8) for _ in range(2)]
        D_sx = [ZF.f32(512) for _ in range(2)]
        rp = ZF.f32(T)
        sgm = [ZF.f32(512) for _ in range(2)]
        pj = [ZF.f32(512) for _ in range(2)]
        OWN = [(2 + 512 * q, 512) for q in range(4)]
        P.add("pool", lambda e: e.dma_start(out=wg, in_=w_gate[:, :, :]), writes=["wg"], dma=dsem("wg"))
        P.add("pool", lambda e: e.dma_start(out=wp, in_=w_proj[:, :, :]), writes=["wp"], dma=dsem("wp"))
        P.add("pool", lambda e: e.dma_start(out=pTb, in_=pT_own[:, :, :]), writes=["pTb"], dma=dsem("pTb"))
        rmsnorm_to(x1, hf, V_GPLE, OWN, rp, "x2", sqf, D_sx)
        for d in range(8):
            for q, (c0, n) in enumerate(OWN):
                s = (d * 4 + q) % 2
                b1, b2 = nextbank(), nextbank()

                def peg(e, d=d, q=q, c0=c0, b1=b1, b2=b2):
                    ins = None
                    for c in range(8):
                        ins = e.matmul(bank(b1), lhsT=wg[:, c, d * 128:(d + 1) * 128], rhs=hf[:, c, c0:c0 + 512], start=(c == 0), stop=(c == 7))
                    for kc in range(2):
                        ins = e.matmul(bank(b2), lhsT=wp[:, kc, d * 128:(d + 1) * 128], rhs=pTb[:, kc, q * 512:(q + 1) * 512], start=(kc == 0), stop=(kc == 1))
                    return ins
                P.add("pe", peg, reads=["wg", "wp", "pTb"] + [("hf", dd, q) for dd in range(8)], writes=[("ps", b1), ("ps", b2)])
                P.add("act", lambda e, s=s, b1=b1: e.activation(out=sgm[s], in_=bank(b1), func=AF.Sigmoid), reads=[("ps", b1)], writes=[("sgm", s)])
                P.add("dve", lambda e, s=s, b2=b2: e.tensor_tensor(out=pj[s], in0=bank(b2), in1=sgm[s], op=ALU.mult), reads=[("ps", b2), ("sgm", s)], writes=[("pj", s)])
                P.add("dve", lambda e, s=s, d=d, c0=c0: e.tensor_tensor(out=x1[:, d, c0:c0 + 512], in0=x1[:, d, c0:c0 + 512], in1=pj[s], op=ALU.add),
                      reads=[("pj", s)], writes=[("x3", d, q)])
        P.barrier()
        rmsnorm_to(x1, None, 0, OWN, rp, "x3", sqf, D_sx)
        stgs = [wo_raw.bitcast(F32)[:, i * 2048:(i + 1) * 2048].rearrange("p (c n) -> p c n", c=4) for i in range(2)]
        for q, (c0, n) in enumerate(OWN):
            for hh in range(2):
                si = (q * 2 + hh) % 2
                for dd in range(4):
                    d = hh * 4 + dd
                    P.add("dve", lambda e, d=d, dd=dd, c0=c0, si=si: e.scalar_tensor_tensor(out=stgs[si][:, dd, :], in0=x1[:, d, c0:c0 + 512], scalar=V(V_GFIN + d),
                                                                                            in1=rp[:, c0:c0 + 512], op0=ALU.mult, op1=ALU.mult),
                          reads=[("rbuf", q), "vec"], writes=[("stg", si, dd)])
                P.add("sp", lambda e, q=q, hh=hh, si=si: e.dma_start(out=outT[:, hh * 4:(hh + 1) * 4, q * 512:(q + 1) * 512], in_=stgs[si]),
                      reads=[("stg", si, dd) for dd in range(4)], dma=dsem(("out", si)))
        P.barrier()
        return finish(nc, P, dma_ids)


V_EPS = EPS


def finish(nc, P, dma_ids):
    from contextlib import ExitStack
    with ExitStack() as es:
        sems = {}
        for e in ("pe", "act", "dve", "pool"):
            sems[e] = es.enter_context(nc.semaphore("s_" + e))
        for _, i in dma_ids.items():
            sems[("d", i)] = es.enter_context(nc.semaphore("d%d" % i))
        block = es.enter_context(nc.Block())

        def emit(ename):
            def body(h):
                for waits, fn, inc in P.ops[ename]:
                    for sk, val in waits:
                        h.wait_ge(sems[sk], val)
                    if fn is not None:
                        fn(h).then_inc(sems[inc[0]], inc[1])
            return body
        block.tensor(emit("pe"))
        block.scalar(emit("act"))
        block.vector(emit("dve"))
        block.gpsimd(emit("pool"))
        block.sync(emit("sp"))
    return nc


def rope_tables():
    pos = np.arange(S, dtype=np.float32)
    inv_freq = (np.float32(10000.0) ** (-np.arange(0, 32, 2, dtype=np.float32) / np.float32(32))).astype(np.float32)
    ang = (pos[:, None] * inv_freq[None, :]).astype(np.float32)
    ang = np.concatenate([ang, ang], axis=-1)
    return np.cos(ang).astype(np.float32), np.sin(ang).astype(np.float32)


def chunked(w):
    K, N = w.shape
    return np.ascontiguousarray(w.reshape(K // 128, 128, N).transpose(1, 0, 2))


def make_in_maps(x, p, norm_mix_g, w_in, conv_w, q_norm_g, w_uq, kv_norm_g, w_ukv, w_o,
                 norm_ffn_g, w_up, ffn_conv_w, ffn_conv_b, w_down, ple_norm_g,
                 w_ple_gate, w_ple_proj, final_norm_g, cores=range(8)):
    f = lambda a: np.asarray(a, dtype=np.float32)
    x, p = f(x), f(p)
    cos, sin = rope_tables()
    shared = {
        "w_in": chunked(f(w_in)[0]), "w_uq": chunked(f(w_uq)[0]), "w_ukv": np.ascontiguousarray(f(w_ukv)[0]),
        "w_o": chunked(f(w_o)[0]), "w_gate": chunked(f(w_ple_gate)[0]), "w_proj": chunked(f(w_ple_proj)[0]),
        "w_down": np.ascontiguousarray(f(w_down)[0].reshape(NFF, 128, 1024)),
        "cosk": np.ascontiguousarray(cos.T), "sink": np.ascontiguousarray(sin.T),
    }
    wu = chunked(f(w_up)[0])
    wu = np.stack([np.concatenate([wu[:, :, j * 128:(j + 1) * 128], wu[:, :, 2816 + j * 128:2816 + (j + 1) * 128]], axis=2)
                   for j in range(NFF)], axis=0)
    shared["w_up"] = np.ascontiguousarray(wu)
    vec = np.zeros((128, NV), np.float32)
    col = lambda v: f(v).reshape(-1, 128).T
    vec[:, V_GMIX:V_GMIX + 8] = col(norm_mix_g[0])
    vec[:, V_GFFN:V_GFFN + 8] = col(norm_ffn_g[0])
    vec[:, V_GPLE:V_GPLE + 8] = col(ple_norm_g[0])
    vec[:, V_GFIN:V_GFIN + 8] = col(final_norm_g)
    vec[:, V_GQ:V_GQ + 2] = col(q_norm_g[0])
    vec[:, V_GKV:V_GKV + 1] = col(kv_norm_g[0])
    cw = f(conv_w)[0]
    for c in range(4):
        for k in range(3):
            vec[:, V_CONVW + 3 * c + k] = cw[k, c * 128:(c + 1) * 128]
    fw, fb = f(ffn_conv_w)[0], f(ffn_conv_b)[0]
    for j in range(2 * NFF):
        for k in range(3):
            vec[:, V_FFNW + 3 * j + k] = fw[k, j * 128:(j + 1) * 128]
        vec[:, V_FFNB + j] = fb[j * 128:(j + 1) * 128]
    maps = []
    for core in cores:
        b, c = core // 4, core % 4
        s0 = c * 2048
        xb = x[b]
        xp = np.zeros((S + 4, 1024), np.float32)
        xp[2:S + 2] = xb
        own = xp[s0:s0 + T]
        posq = np.clip(np.arange(s0 - 2, s0 - 2 + T), 0, S - 1)
        v = vec.copy()
        v[:, V_HM] = 1.0 if c > 0 else 0.0
        v[:, V_HM + 1] = 1.0 if c < 3 else 0.0
        m = dict(shared)
        m["xT_full"] = chunked(np.ascontiguousarray(xb.T))
        m["xT_own"] = chunked(np.ascontiguousarray(own.T))
        m["pT_own"] = chunked(np.ascontiguousarray(p[0, b, s0:s0 + 2048].T))
        m["vecs"] = v
        m["cosq"] = np.ascontiguousarray(cos[posq].T)
        m["sinq"] = np.ascontiguousarray(sin[posq].T)
        maps.append(m)
    return maps


_NC_CACHE = {}


def kernel(**inputs):
    if "nc" not in _NC_CACHE:
        _NC_CACHE["nc"] = build_program()
    nc = _NC_CACHE["nc"]
    maps = make_in_maps(**inputs)
    res = run_bass_kernel_spmd(nc, maps, core_ids=list(range(8)))
    out = np.empty((2, S, 1024), np.float32)
    for core in range(8):
        b, c = core // 4, core % 4
        o = res.results[core]["outT"]
        out[b, c * 2048:(c + 1) * 2048] = o.transpose(2, 1, 0).reshape(2048, 1024)
    return out
```
